# Optimizing a Trainium2 kernel written in Bass

```python
import math
import jax, jax.numpy as jnp
from jax import lax
import numpy as np

D_MODEL = 1024
BATCH = 1
SEQ = 16384
DEPTH = 2

GRID_W = 64
CTX_LEN = 256
ALPHA = (2 * DEPTH) ** 0.25
BETA = (8 * DEPTH) ** -0.25
LN_EPS = 1e-6
D_FF = ((math.ceil(8 * D_MODEL / 3) + 255) // 256) * 256

RWKV_HEAD = 64
D_A = 3 * D_MODEL // 4
H_A = D_A // RWKV_HEAD
DECAY_LORA = 64
ICL_LORA = 64
GATE_LORA = 128
GN_EPS = 64e-5
D_B = D_MODEL - D_A
POOL_WINDOWS = (2, 4, 8, 16)
POOL_GROUP = D_B // len(POOL_WINDOWS)
C_RW = 3 * D_A + 2 * DECAY_LORA + 2 * ICL_LORA + GATE_LORA
D_IN_EVEN = C_RW + D_B
NA_HEAD = 64
H_C = D_MODEL // NA_HEAD
NA_KH = 8
NA_KW = 16

N_EVEN = (DEPTH + 1) // 2
N_ODD = DEPTH // 2

kernel_name = 'rwkv7_pool_natten_hybrid_dit'


def layer_norm(x, g, b):
    xf = x.astype(jnp.float32)
    mu = xf.mean(-1, keepdims=True)
    var = jnp.mean(jnp.square(xf - mu), -1, keepdims=True)
    return ((xf - mu) * lax.rsqrt(var + LN_EPS)).astype(x.dtype) * g + b


def swiglu(h, w1, w3, w2):
    return (jax.nn.silu(h @ w1) * (h @ w3)) @ w2


def centred_shift_mix(p, mu):
    zero = jnp.zeros_like(p[:, :1])
    prev = jnp.concatenate([zero, p[:, :-1]], axis=1)
    nxt = jnp.concatenate([p[:, 1:], zero], axis=1)
    return p + (prev - p) * mu[0] + (nxt - p) * mu[1]


def rwkv_prepare(p, mu, w0, w2, a0, a2, g2, k_k, k_a):
    B, T, _ = p.shape
    p = centred_shift_mix(p, mu)
    s1 = D_A; s2 = 2 * D_A; s3 = 3 * D_A; s4 = s3 + 2 * DECAY_LORA; s5 = s4 + 2 * ICL_LORA
    r, k, v, wd, ad, gd = jnp.split(p, [s1, s2, s3, s4, s5], axis=-1)
    wd = wd.reshape(B, T, 2, DECAY_LORA)
    ad = ad.reshape(B, T, 2, ICL_LORA)
    f32 = jnp.float32
    logw = -jax.nn.softplus(-(w0 + jnp.einsum('btdr,drc->btdc', jnp.tanh(wd), w2)).astype(f32)) - 0.5
    decay = jnp.exp(-jnp.exp(logw))
    a = jax.nn.sigmoid((a0 + jnp.einsum('btdr,drc->btdc', ad, a2)).astype(f32))
    g = jax.nn.sigmoid(gd) @ g2
    r = r.astype(f32); k = k.astype(f32); v = v.astype(f32)
    kk = (k * k_k).reshape(B, T, H_A, RWKV_HEAD)
    kk = kk / jnp.maximum(jnp.linalg.norm(kk, axis=-1, keepdims=True), 1e-12)
    kd = k[:, :, None, :] * (1.0 + (a - 1.0) * k_a)
    bd = kk.reshape(B, T, 1, D_A) * a
    hs = (B, T, 2, H_A, RWKV_HEAD)
    return (r.reshape(B, T, H_A, RWKV_HEAD), v.reshape(B, T, H_A, RWKV_HEAD), kk,
            decay.reshape(hs), kd.reshape(hs), bd.reshape(hs), g)


def wkv_scan(S0, r, w, k, v, kk, b, reverse):
    def step(S, inp):
        r_t, w_t, k_t, v_t, kk_t, b_t = inp
        sa = -jnp.einsum('bhij,bhj->bhi', S, kk_t)
        S = S * w_t[:, :, None, :] + sa[..., :, None] * b_t[:, :, None, :] + v_t[..., :, None] * k_t[:, :, None, :]
        return S, jnp.einsum('bhij,bhj->bhi', S, r_t)
    xs = tuple(jnp.swapaxes(t, 0, 1) for t in (r, w, k, v, kk, b))
    S, ys = lax.scan(step, S0, xs, reverse=reverse)
    return S, jnp.swapaxes(ys, 0, 1)


def rwkv_readout(y, r, v, kd, g, r_k, lnx_g, lnx_b):
    B, T = y.shape[:2]
    mu = y.mean(-1, keepdims=True)
    var = jnp.mean(jnp.square(y - mu), -1, keepdims=True)
    yn = ((y - mu) * lax.rsqrt(var + GN_EPS)).reshape(B, T, D_A) * lnx_g + lnx_b
    coef = jnp.sum(r[:, :, None] * kd * r_k, axis=(2, 4))
    bonus = (coef[..., None] * v).reshape(B, T, D_A)
    return ((yn + bonus) * g).astype(g.dtype)


def multiscale_pool(p, pool_w, pool_scale):
    B, T, _ = p.shape
    pf = p.astype(jnp.float32)
    cs = jnp.concatenate([jnp.zeros((B, 1, D_B), jnp.float32), jnp.cumsum(pf, axis=1)], axis=1)
    t = jnp.arange(T)
    groups = []
    for gi, win in enumerate(POOL_WINDOWS):
        lo = jnp.clip(t - win // 2, 0, T)
        hi = jnp.clip(t + win // 2, 0, T)
        sl = slice(gi * POOL_GROUP, (gi + 1) * POOL_GROUP)
        cg = cs[:, :, sl]
        mean = (cg[:, hi] - cg[:, lo]) / (hi - lo).astype(jnp.float32)[None, :, None]
        groups.append(mean - pf[:, :, sl])
    pooled = jnp.stack(groups, axis=2).astype(p.dtype)
    y = jnp.einsum('btgc,gcd->btgd', pooled, pool_w).reshape(B, T, D_B)
    return y * pool_scale


def rwkv_pool_mixer(h, hc, want_ctx, w_in, shift_mu, w0, w2, a0, a2, g2, k_k, k_a, r_k,
                    lnx_g, lnx_b, pool_w, pool_scale, w_out):
    B = h.shape[0]
    p_lat = h @ w_in
    p_ctx = hc @ w_in
    lat = rwkv_prepare(p_lat[..., :C_RW], shift_mu, w0, w2, a0, a2, g2, k_k, k_a)
    cx = rwkv_prepare(p_ctx[..., :C_RW], shift_mu, w0, w2, a0, a2, g2, k_k, k_a)
    S0 = jnp.zeros((B, H_A, RWKV_HEAD, RWKV_HEAD), jnp.float32)
    y_lat = 0.0
    y_ctx = 0.0
    for d in range(2):
        rev = d == 1
        S_c, yc = wkv_scan(S0, cx[0], cx[3][:, :, d], cx[4][:, :, d], cx[1], cx[2], cx[5][:, :, d], rev)
        _, yl = wkv_scan(S_c, lat[0], lat[3][:, :, d], lat[4][:, :, d], lat[1], lat[2], lat[5][:, :, d], rev)
        y_lat = y_lat + yl
        if want_ctx:
            y_ctx = y_ctx + yc
    a_lat = rwkv_readout(y_lat, lat[0], lat[1], lat[4], lat[6], r_k, lnx_g, lnx_b)
    b_lat = multiscale_pool(p_lat[..., C_RW:], pool_w, pool_scale)
    out_lat = jnp.concatenate([a_lat, b_lat], axis=-1) @ w_out
    if not want_ctx:
        return out_lat, None
    a_ctx = rwkv_readout(y_ctx, cx[0], cx[1], cx[4], cx[6], r_k, lnx_g, lnx_b)
    b_ctx = multiscale_pool(p_ctx[..., C_RW:], pool_w, pool_scale)
    out_ctx = jnp.concatenate([a_ctx, b_ctx], axis=-1) @ w_out
    return out_lat, out_ctx


def neighbourhood_attention(q, k, v, kc, vc, rpb):
    B, rows = q.shape[:2]
    kh = min(NA_KH, rows)
    kw = NA_KW
    n_loc = kh * kw
    scale = NA_HEAD ** -0.5
    cols = jnp.arange(GRID_W)
    col_start = jnp.clip(cols - kw // 2, 0, GRID_W - kw)
    col_idx = col_start[:, None] + jnp.arange(kw)[None, :]
    col_off = col_idx - cols[:, None] + (NA_KW - 1)

    def row_block(r):
        sr = jnp.clip(r - kh // 2, 0, rows - kh)
        qr = lax.dynamic_index_in_dim(q, r, axis=1, keepdims=False)
        kr = lax.dynamic_slice_in_dim(k, sr, kh, axis=1)
        vr = lax.dynamic_slice_in_dim(v, sr, kh, axis=1)
        kg = jnp.take(kr, col_idx, axis=2).transpose(0, 2, 1, 3, 4, 5).reshape(B, GRID_W, n_loc, H_C, NA_HEAD)
        vg = jnp.take(vr, col_idx, axis=2).transpose(0, 2, 1, 3, 4, 5).reshape(B, GRID_W, n_loc, H_C, NA_HEAD)
        row_off = sr - r + jnp.arange(kh) + (NA_KH - 1)
        bias = rpb[:, row_off[:, None, None], col_off[None, :, :]]
        bias = bias.transpose(0, 2, 1, 3).reshape(H_C, GRID_W, n_loc).astype(jnp.float32)
        s_loc = jnp.einsum('bqhd,bqnhd->bhqn', qr, kg).astype(jnp.float32) * scale + bias
        s_ctx = jnp.einsum('bqhd,bnhd->bhqn', qr, kc).astype(jnp.float32) * scale
        p = jax.nn.softmax(jnp.concatenate([s_loc, s_ctx], axis=-1), axis=-1).astype(v.dtype)
        return (jnp.einsum('bhqn,bqnhd->bqhd', p[..., :n_loc], vg)
                + jnp.einsum('bhqn,bnhd->bqhd', p[..., n_loc:], vc))

    o = lax.map(row_block, jnp.arange(rows))
    return jnp.moveaxis(o, 0, 1)


def na_mixer(h, hc, want_ctx, w_in, rpb, w_out):
    B, L, D = h.shape
    rows = L // GRID_W
    n_ctx = hc.shape[1]
    q, k, v = jnp.split(h @ w_in, 3, axis=-1)
    kc, vc = jnp.split(hc @ w_in[:, D:], 2, axis=-1)
    kc = kc.reshape(B, n_ctx, H_C, NA_HEAD)
    vc = vc.reshape(B, n_ctx, H_C, NA_HEAD)
    gs = (B, rows, GRID_W, H_C, NA_HEAD)
    o = neighbourhood_attention(q.reshape(gs), k.reshape(gs), v.reshape(gs), kc, vc, rpb)
    out_lat = o.reshape(B, L, D) @ w_out
    if not want_ctx:
        return out_lat, None
    qc = (hc @ w_in[:, :D]).reshape(B, n_ctx, H_C, NA_HEAD)
    s = jnp.einsum('bqhd,bkhd->bhqk', qc, kc).astype(jnp.float32) * NA_HEAD ** -0.5
    p = jax.nn.softmax(s, axis=-1).astype(vc.dtype)
    oc = jnp.einsum('bhqk,bkhd->bqhd', p, vc).reshape(B, n_ctx, D)
    return out_lat, oc @ w_out


def setup_inputs(seed: int = 0) -> dict:
    key = jax.random.key(seed)
    ks = jax.random.split(key, 32)
    D = D_MODEL
    f32 = jnp.float32

    def n(k, shape, s):
        return jax.random.normal(k, shape, f32) * s

    return {
        'x': n(ks[0], (BATCH, SEQ, D), 1.0),
        'c': n(ks[1], (BATCH, D), 1.0),
        'ctx': n(ks[2], (BATCH, CTX_LEN, D), 1.0),
        'c_ctx': n(ks[3], (D,), 1.0),
        'ada_w': n(ks[4], (DEPTH, D, 6 * D), 0.5 * D ** -0.5),
        'ada_b': n(ks[5], (DEPTH, 6 * D), 0.02),
        'ln_g': 1.0 + n(ks[6], (DEPTH, 2, D), 0.05),
        'ln_b': n(ks[7], (DEPTH, 2, D), 0.02),
        'ffn_w1': n(ks[8], (DEPTH, D, D_FF), D ** -0.5),
        'ffn_w3': n(ks[9], (DEPTH, D, D_FF), D ** -0.5),
        'ffn_w2': n(ks[10], (DEPTH, D_FF, D), BETA * D_FF ** -0.5),
        'ev_w_in': n(ks[11], (N_EVEN, D, D_IN_EVEN), D ** -0.5),
        'ev_shift_mu': jax.random.uniform(ks[12], (N_EVEN, 2, C_RW), f32, 0.0, 0.5),
        'ev_w0': jax.random.uniform(ks[13], (N_EVEN, 2, D_A), f32, -6.5, -1.5),
        'ev_w2': n(ks[14], (N_EVEN, 2, DECAY_LORA, D_A), 0.5 * DECAY_LORA ** -0.5),
        'ev_a0': n(ks[15], (N_EVEN, 2, D_A), 0.1),
        'ev_a2': n(ks[16], (N_EVEN, 2, ICL_LORA, D_A), 0.5 * ICL_LORA ** -0.5),
        'ev_g2': n(ks[17], (N_EVEN, GATE_LORA, D_A), GATE_LORA ** -0.5),
        'ev_k_k': 0.85 + n(ks[18], (N_EVEN, D_A), 0.05),
        'ev_k_a': 1.0 + n(ks[19], (N_EVEN, D_A), 0.05),
        'ev_r_k': n(ks[20], (N_EVEN, H_A, RWKV_HEAD), 0.1),
        'ev_lnx_g': 1.0 + n(ks[21], (N_EVEN, D_A), 0.05),
        'ev_lnx_b': n(ks[22], (N_EVEN, D_A), 0.02),
        'ev_pool_w': n(ks[23], (N_EVEN, len(POOL_WINDOWS), POOL_GROUP, POOL_GROUP), POOL_GROUP ** -0.5),
        'ev_pool_scale': 1.0 + n(ks[24], (N_EVEN, D_B), 0.1),
        'ev_w_out': n(ks[25], (N_EVEN, D, D), BETA * D ** -0.5),
        'od_w_in': n(ks[26], (N_ODD, D, 3 * D), D ** -0.5),
        'od_rpb': n(ks[27], (N_ODD, H_C, 2 * NA_KH - 1, 2 * NA_KW - 1), 0.05),
        'od_w_out': n(ks[28], (N_ODD, D, D), BETA * D ** -0.5),
    }


def reference(x, c, ctx, c_ctx, ada_w, ada_b, ln_g, ln_b, ffn_w1, ffn_w3, ffn_w2,
              ev_w_in, ev_shift_mu, ev_w0, ev_w2, ev_a0, ev_a2, ev_g2, ev_k_k, ev_k_a, ev_r_k,
              ev_lnx_g, ev_lnx_b, ev_pool_w, ev_pool_scale, ev_w_out,
              od_w_in, od_rpb, od_w_out):
    xc = ctx
    for i in range(DEPTH):
        last = i == DEPTH - 1
        want_ctx = not last
        j = i // 2
        mod = jax.nn.silu(c) @ ada_w[i] + ada_b[i]
        mod_c = jax.nn.silu(c_ctx) @ ada_w[i] + ada_b[i]
        sh_m, sc_m, g_m, sh_f, sc_f, g_f = jnp.split(mod[:, None, :], 6, axis=-1)
        shc_m, scc_m, gc_m, shc_f, scc_f, gc_f = jnp.split(mod_c, 6, axis=-1)
        h = x * (1.0 + sc_m) + sh_m
        hc = xc * (1.0 + scc_m) + shc_m
        if i % 2 == 0:
            y, yc = rwkv_pool_mixer(h, hc, want_ctx, ev_w_in[j], ev_shift_mu[j], ev_w0[j], ev_w2[j],
                                    ev_a0[j], ev_a2[j], ev_g2[j], ev_k_k[j], ev_k_a[j], ev_r_k[j],
                                    ev_lnx_g[j], ev_lnx_b[j], ev_pool_w[j], ev_pool_scale[j], ev_w_out[j])
        else:
            y, yc = na_mixer(h, hc, want_ctx, od_w_in[j], od_rpb[j], od_w_out[j])
        x = layer_norm(ALPHA * x + g_m * y, ln_g[i, 0], ln_b[i, 0])
        h = x * (1.0 + sc_f) + sh_f
        x = layer_norm(ALPHA * x + g_f * swiglu(h, ffn_w1[i], ffn_w3[i], ffn_w2[i]), ln_g[i, 1], ln_b[i, 1])
        if want_ctx:
            xc = layer_norm(ALPHA * xc + gc_m * yc, ln_g[i, 0], ln_b[i, 0])
            hc = xc * (1.0 + scc_f) + shc_f
            xc = layer_norm(ALPHA * xc + gc_f * swiglu(hc, ffn_w1[i], ffn_w3[i], ffn_w2[i]), ln_g[i, 1], ln_b[i, 1])
    return x
```

```python
import contextlib
import numpy as np
import concourse.bass as bass
import concourse.mybir as mybir

F32 = mybir.dt.float32
BF16 = mybir.dt.bfloat16
AF = mybir.ActivationFunctionType
ALU = mybir.AluOpType
AX = mybir.AxisListType

ENGS = ("pe", "act", "dve", "pool", "sp")
N_DMA_SEMS = 24
SAME_ENGINE_SYNC = True


class Prog:
    def __init__(self, nc):
        self.nc = nc
        self.stack = contextlib.ExitStack()
        self.count = {e: 0 for e in ENGS}
        self.clock = {e: {} for e in ENGS}
        self.sem = {}
        for e in ENGS:
            self.sem[e] = self.stack.enter_context(nc.semaphore("s_" + e))
        self.dsems = [self.stack.enter_context(nc.semaphore("d%d" % i)) for i in range(N_DMA_SEMS)]
        self.dcount = [0] * N_DMA_SEMS
        self.dlast = [None] * N_DMA_SEMS
        self.drr = 0
        self.res = {}
        self.n_wait = 0
        self.out_dmas = []

    def sb(self, name, shape, dt=F32):
        self.uid = getattr(self, "uid", 0) + 1
        return self.stack.enter_context(self.nc.sbuf_tensor("%s_s%d" % (name, self.uid), list(shape), dt))

    def ps(self, name, shape, dt=F32):
        self.uid = getattr(self, "uid", 0) + 1
        return self.stack.enter_context(self.nc.psum_tensor("%s_p%d" % (name, self.uid), list(shape), dt))

    def _semof(self, key):
        if isinstance(key, tuple):
            return self.dsems[key[1]]
        return self.sem[key]

    def _need(self, eng, dep, waits):
        if dep is None:
            return
        key, val, vc = dep
        if not SAME_ENGINE_SYNC and key == eng:
            return
        if key == "pe" and eng == "pe":
            return
        if self.clock[eng].get(key, 0) >= val:
            return
        cur = waits.get(key)
        if cur is None or cur[0] < val:
            waits[key] = (val, vc)

    def _deps(self, eng, reads, writes, extra=()):
        waits = {}
        for r in reads:
            st = self.res.get(r)
            if st is not None:
                self._need(eng, st["w"], waits)
        for w in writes:
            st = self.res.get(w)
            if st is not None:
                self._need(eng, st["w"], waits)
                for rd in st["r"]:
                    self._need(eng, rd, waits)
        for d in extra:
            self._need(eng, d, waits)
        items = list(waits.items())
        final = []
        for key, (val, vc) in items:
            implied = False
            for k2, (v2, vc2) in items:
                if k2 != key and vc2.get(key, 0) >= val:
                    implied = True
                    break
            if not implied:
                final.append((key, val))
        ck = self.clock[eng]
        for key, (val, vc) in items:
            for k, v in vc.items():
                if ck.get(k, 0) < v:
                    ck[k] = v
        return final

    def _record(self, me, reads, writes):
        for r in reads:
            st = self.res.setdefault(r, {"w": None, "r": []})
            st["r"].append(me)
        for w in writes:
            self.res[w] = {"w": me, "r": []}

    def _eng(self, e):
        nc = self.nc
        return {"pe": nc.tensor, "act": nc.scalar, "dve": nc.vector, "pool": nc.gpsimd, "sp": nc.sync}[e]

    def op(self, eng, fn, reads=(), writes=()):
        waits = self._deps(eng, reads, writes)
        self.count[eng] += 1
        val = self.count[eng]
        vc = dict(self.clock[eng])
        vc[eng] = val
        me = (eng, val, vc)
        self._record(me, reads, writes)
        self.n_wait += len(waits)
        eo = self._eng(eng)
        for key, v in waits:
            eo.wait_ge(self._semof(key), v)
        fn(eo).then_inc(self.sem[eng], 1)
        return me

    def dma(self, out, in_, reads=(), writes=(), q="sp", is_output=False, **kw):
        s = self.drr
        self.drr = (self.drr + 1) % N_DMA_SEMS
        extra = [self.dlast[s]] if self.dlast[s] is not None else []
        waits = self._deps(q, reads, writes, extra)
        self.dcount[s] += 16
        key = ("d", s)
        vc = dict(self.clock[q])
        vc[key] = self.dcount[s]
        me = (key, self.dcount[s], vc)
        self.dlast[s] = me
        self._record(me, reads, writes)
        self.n_wait += len(waits)
        eo = self._eng(q)
        for k, v in waits:
            eo.wait_ge(self._semof(k), v)
        eo.dma_start(out=out, in_=in_, **kw).then_inc(self.dsems[s], 16)
        if is_output:
            self.out_dmas.append(me)
        return me

    def barrier(self):
        deps = []
        for e in ENGS:
            if self.count[e] > 0:
                deps.append((e, self.count[e], {e: self.count[e]}))
        for d in self.dlast:
            if d is not None:
                deps.append((d[0], d[1], {d[0]: d[1]}))
        for e in ENGS:
            waits = {}
            for d in deps:
                if d[0] == e and e != "sp":
                    if not SAME_ENGINE_SYNC or e == "pe":
                        continue
                self._need(e, d, waits)
            eo = self._eng(e)
            for key, (val, vc) in waits.items():
                eo.wait_ge(self._semof(key), val)
                self.clock[e][key] = max(self.clock[e].get(key, 0), val)
        self.res = {}

    def finish(self):
        for d in self.dlast:
            if d is not None and self.clock["sp"].get(d[0], 0) < d[1]:
                self.nc.sync.wait_ge(self._semof(d[0]), d[1])
        self.stack.close()

    @contextlib.contextmanager
    def scope(self):
        old = self.stack
        self.stack = contextlib.ExitStack()
        try:
            yield
        finally:
            self.barrier()
            self.stack.close()
            self.stack = old


D = 1024
DFF = 2816
NFC = 22
ALPHA = 4.0 ** 0.25
EPSP = 1e-6 / (ALPHA * ALPHA)
NEG = -30000.0
_eng_rr = [0]


class Cx:
    def __init__(self, p, nc, ident_ap, TT):
        self.p, self.nc, self.TT = p, nc, TT
        self.pb = [p.ps("pb%d" % i, [128, 512]) for i in range(8)]
        self.ident = p.sb("ident", [128, 128])
        self.ones = p.sb("ones", [128, 128])
        self.onesb = p.sb("onesb", [128, 64], BF16)
        p.dma(self.ident[:], ident_ap, writes=["ident"])
        p.op("pool", lambda e: e.memset(self.ones[:], 1.0), [], ["ones"])
        p.op("pool", lambda e: e.memset(self.onesb[:], 1.0), [], ["onesb"])
        self.wi = 0

    def alloc_work(self, wst_cols, TT):
        p = self.p
        self.TT = TT
        self.wst = [p.sb("wst%d" % i, [128, wst_cols]) for i in range(2)]
        self.wbf = [p.sb("wbf%d" % i, [128, wst_cols], BF16) for i in range(2)]
        self.tb = p.sb("tb", [128, 8, TT])
        self.sq = [p.sb("sq%d" % i, [128, TT]) for i in range(2)]
        self.st = [p.sb("st%d" % i, [128, TT]) for i in range(5)]
        self.io = [p.sb("io%d" % i, [128, 1024]) for i in range(2)]
        self.ioi = 0

    def wblock(self, w_ap, kc, col0, ncols=128, cast=True):
        p = self.p
        i = self.wi
        self.wi ^= 1
        st = self.wst[i][:, :kc * ncols].rearrange("p (k n) -> p k n", n=ncols)
        src = w_ap[:, col0:col0 + ncols].rearrange("(k p) n -> p k n", p=128)
        p.dma(st, src, writes=[("wst", i)])
        if not cast:
            return st, ("wst", i)
        bf = self.wbf[i][:, :kc * ncols].rearrange("p (k n) -> p k n", n=ncols)
        p.op("pool", lambda e: e.tensor_copy(out=bf, in_=st), [("wst", i)], [("wbf", i)])
        return bf, ("wbf", i)

    def load_T(self, rows_ap, ntok, dst, dkey, t0=0):
        p = self.p
        s0 = 0
        while s0 < ntok:
            n = min(128, ntok - s0)
            io = self.io[self.ioi]
            iok = ("io", self.ioi)
            self.ioi ^= 1
            p.dma(io[0:n, :], rows_ap[s0:s0 + n, :], writes=[iok])
            for hb in range(2):
                bank = self.pb[6 + hb]
                bk = "pb%d" % (6 + hb)
                for c4 in range(4):
                    c = hb * 4 + c4
                    p.op("pe", lambda e, c=c, c4=c4, bank=bank, io=io, n=n: e.transpose(bank[:, c4 * 128:c4 * 128 + n], io[0:n, c * 128:(c + 1) * 128], self.ident[0:n, 0:n]),
                         [iok, "ident"], [bk])
                out = dst[:, hb * 4:hb * 4 + 4, t0 + s0:t0 + s0 + n]
                src = bank[:].rearrange("p (a b) -> p a b", b=128)[:, :, 0:n]
                p.op("act" if hb == 0 else "dve",
                     (lambda e, out=out, src=src: e.copy(out=out, in_=src)) if hb == 0 else (lambda e, out=out, src=src: e.tensor_copy(out=out, in_=src)),
                     [bk], [(dkey, hb * 4 + i) for i in range(4)])
            s0 += n

    def store_T(self, src, skey, ntok, rows_ap, t0=0, is_output=True):
        p = self.p
        for s in range(ntok // 128):
            io = self.io[self.ioi]
            iok = ("io", self.ioi)
            self.ioi ^= 1
            for hb in range(2):
                bank = self.pb[6 + hb]
                bk = "pb%d" % (6 + hb)
                for c4 in range(4):
                    c = hb * 4 + c4
                    p.op("pe", lambda e, c=c, c4=c4, bank=bank: e.transpose(bank[:, c4 * 128:(c4 + 1) * 128], src[:, c, t0 + s * 128:t0 + (s + 1) * 128], self.ident[:]),
                         [(skey, c), "ident"], [bk])
                if hb == 0:
                    p.op("act", lambda e, io=io, bank=bank: e.copy(out=io[:, 0:512], in_=bank[:]), [bk], [iok])
                else:
                    p.op("dve", lambda e, io=io, bank=bank: e.tensor_copy(out=io[:, 512:1024], in_=bank[:]), [bk], [iok])
            p.dma(rows_ap[s * 128:(s + 1) * 128, :], io[:], reads=[iok], is_output=is_output)

    def mod_cols(self, cc_ap, adaw_ap, adab_ap, v0, nv, name, dm):
        p = self.p
        cc = p.sb(name + "_cc", [128, 8, 2])
        sg = p.sb(name + "_sg", [128, 8, 2])
        ab = p.sb(name + "_ab", [128, 48])
        p.dma(cc[:], cc_ap, writes=[name + "cc"])
        p.dma(ab[:], adab_ap, writes=[name + "ab"])
        p.op("act", lambda e: e.activation(out=sg[:], in_=cc[:], func=AF.Sigmoid), [name + "cc"], [name + "sg"])
        p.op("dve", lambda e: e.tensor_tensor(out=cc[:], in0=cc[:], in1=sg[:], op=ALU.mult), [name + "cc", name + "sg"], [name + "cc"])
        bank = self.pb[5]
        for m in range(nv * 8):
            wb, wk = self.wblock(adaw_ap, 8, (v0 * 8 + m) * 128, cast=False)
            for k in range(8):
                p.op("pe", lambda e, m=m, k=k, wb=wb: e.matmul(bank[:, 2 * m:2 * m + 2], lhsT=wb[:, k, :], rhs=cc[:, k, :], start=(k == 0), stop=(k == 7)),
                     [wk, name + "cc"], ["pb5"])
        bsrc = ab[:, v0 * 8:(v0 + nv) * 8]
        p.op("dve", lambda e: e.tensor_tensor(out=dm[:], in0=bank[:, 0:nv * 16].rearrange("p (m t) -> p m t", t=2),
                                              in1=bsrc.unsqueeze(2).to_broadcast([128, nv * 8, 2]), op=ALU.add),
             ["pb5", name + "ab"], [name])
        return dm

    def tail(self, T, emit_y, xT, xkey, gcol, lng, lnb, hT=None, hkey=None, sc1=None, sh=None, modkeys=()):
        p = self.p
        tb = self.tb
        for c in range(8):
            yb = self.pb[4 + (c % 2)]
            yk = "pb%d" % (4 + c % 2)
            emit_y(c, yb[:, :T], yk)
            p.op("dve", lambda e, c=c, yb=yb: e.scalar_tensor_tensor(out=tb[:, c, :T], in0=yb[:, :T], scalar=gcol(c), in1=xT[:, c, :], op0=ALU.mult, op1=ALU.add),
                 [yk, (xkey, c)] + list(modkeys), [("tb", c)])
            sq = self.sq[c % 2]
            p.op("pool", lambda e, c=c, sq=sq: e.tensor_tensor(out=sq[:, :T], in0=tb[:, c, :T], in1=tb[:, c, :T], op=ALU.mult), [("tb", c)], [("sq", c % 2)])
            p.op("pe", lambda e, c=c: e.matmul(self.pb[6][:, :T], lhsT=self.ones[:], rhs=tb[:, c, :T], start=(c == 0), stop=(c == 7)), [("tb", c), "ones"], ["pb6"])
            p.op("pe", lambda e, c=c, sq=sq: e.matmul(self.pb[7][:, :T], lhsT=self.ones[:], rhs=sq[:, :T], start=(c == 0), stop=(c == 7)), [("sq", c % 2), "ones"], ["pb7"])
        mean, msq, var, std, rstd = [t[:, :T] for t in self.st]
        p.op("act", lambda e: e.mul(out=mean, in_=self.pb[6][:, :T], mul=1.0 / D), ["pb6"], ["st0"])
        p.op("dve", lambda e: e.tensor_tensor(out=msq, in0=mean, in1=mean, op=ALU.mult), ["st0"], ["st1"])
        p.op("dve", lambda e: e.scalar_tensor_tensor(out=var, in0=self.pb[7][:, :T], scalar=1.0 / D, in1=msq, op0=ALU.mult, op1=ALU.subtract), ["pb7", "st1"], ["st2"])
        p.op("dve", lambda e: e.tensor_scalar(out=var, in0=var, scalar1=EPSP, scalar2=None, op0=ALU.add), ["st2"], ["st2"])
        p.op("act", lambda e: e.activation(out=std, in_=var, func=AF.Sqrt), ["st2"], ["st3"])
        p.op("dve", lambda e: e.reciprocal(out=rstd, in_=std), ["st3"], ["st4"])
        for c in range(8):
            p.op("pool", lambda e, c=c: e.tensor_tensor(out=tb[:, c, :T], in0=tb[:, c, :T], in1=mean, op=ALU.subtract), [("tb", c), "st0"], [("tb", c)])
            p.op("dve", lambda e, c=c: e.tensor_tensor(out=tb[:, c, :T], in0=tb[:, c, :T], in1=rstd, op=ALU.mult), [("tb", c), "st4"], [("tb", c)])
            p.op("act", lambda e, c=c: e.activation(out=xT[:, c, :], in_=tb[:, c, :T], func=AF.Identity, scale=lng(c), bias=lnb(c)),
                 [("tb", c), "lnc"], [(xkey, c)])
            if hT is not None:
                p.op("dve", lambda e, c=c: e.tensor_scalar(out=hT[:, c, :], in0=xT[:, c, :], scalar1=sc1(c), scalar2=sh(c), op0=ALU.mult, op1=ALU.add),
                     [(xkey, c)] + list(modkeys), [(hkey, c)])

    def modulate(self, T, xT, xkey, hT, hkey, sc1, sh, modkeys=()):
        for c in range(8):
            self.p.op("dve" if c % 2 else "pool", lambda e, c=c: e.tensor_scalar(out=hT[:, c, :], in0=xT[:, c, :], scalar1=sc1(c), scalar2=sh(c), op0=ALU.mult, op1=ALU.add),
                      [(xkey, c)] + list(modkeys), [(hkey, c)])

    def ffn_up(self, T, hT, hkey, w1, w3, actT):
        p = self.p
        for f in range(NFC):
            b1, k1 = self.wblock(w1, 8, f * 128)
            b3, k3 = self.wblock(w3, 8, f * 128)
            u1 = self.pb[0 + 2 * (f % 2)]
            u3 = self.pb[1 + 2 * (f % 2)]
            n1 = "pb%d" % (0 + 2 * (f % 2))
            n3 = "pb%d" % (1 + 2 * (f % 2))
            for k in range(8):
                p.op("pe", lambda e, k=k, b1=b1, u1=u1: e.matmul(u1[:, :T], lhsT=b1[:, k, :], rhs=hT[:, k, :], start=(k == 0), stop=(k == 7)), [k1, (hkey, k)], [n1])
            for k in range(8):
                p.op("pe", lambda e, k=k, b3=b3, u3=u3: e.matmul(u3[:, :T], lhsT=b3[:, k, :], rhs=hT[:, k, :], start=(k == 0), stop=(k == 7)), [k3, (hkey, k)], [n3])
            sl = self.sq[f % 2]
            p.op("act", lambda e, u1=u1, sl=sl: e.activation(out=sl[:, :T], in_=u1[:, :T], func=AF.Silu), [n1], [("sq", f % 2)])
            p.op("dve", lambda e, u3=u3, sl=sl, f=f: e.tensor_tensor(out=actT[:, f, :T], in0=u3[:, :T], in1=sl[:, :T], op=ALU.mult), [n3, ("sq", f % 2)], [("actT", f)])

    def ffn_down_emit(self, T, w2, actT):
        def emit(c, ps, pk):
            b2, k2 = self.wblock(w2, NFC, c * 128)
            for f in range(NFC):
                self.p.op("pe", lambda e, f=f, b2=b2: e.matmul(ps, lhsT=b2[:, f, :], rhs=actT[:, f, :T], start=(f == 0), stop=(f == NFC - 1)), [k2, ("actT", f)], [pk])
        return emit

    def proj_emit(self, w_ap, col_base, rhsT, rkey, T, kc=8):
        def emit(c, ps, pk):
            b, bk = self.wblock(w_ap, kc, col_base + c * 128)
            for k in range(kc):
                self.p.op("pe", lambda e, k=k, b=b: e.matmul(ps, lhsT=b[:, k, :], rhs=rhsT[:, k, :T], start=(k == 0), stop=(k == kc - 1)), [bk, (rkey, k)], [pk])
        return emit


def build_C():
    nc = bass.Bass("TRN2", target_bir_lowering=False)
    dt = lambda n, s, k="ExternalInput", d=F32: nc.dram_tensor(n, list(s), d, kind=k).ap()
    x1h = dt("x1h", [2560, D]); xc1 = dt("xc1", [256, D]); cc = dt("cc", [128, 8, 2])
    adaw = dt("adaw", [D, 6 * D]); adab = dt("adab", [128, 48]); lnc = dt("lnc", [128, 4, 8])
    w1 = dt("w1", [D, DFF]); w3 = dt("w3", [D, DFF]); w2 = dt("w2", [DFF, D])
    win = dt("win", [D, 3 * D]); wout = dt("wout", [D, D])
    tabd = dt("tab", [128, 15 * 16 * 64]); rmd = dt("rowmask", [128, 8 * 6 * 4]); identd = dt("ident", [128, 128])
    out = dt("out", [2048, D], "ExternalOutput")
    xmid = dt("xmid", [128, 8, 2048], "Internal")
    p = Prog(nc)
    cx = Cx(p, nc, identd, 512)
    dm = cx_mod = None
    ln = p.sb("ln", [128, 4, 8])
    p.dma(ln[:], lnc, writes=["lnc"])
    dm = p.sb("mod_dm", [128, 48, 2])
    with p.scope():
        cx.alloc_work(8 * 128, 256)
        cx.mod_cols(cc, adaw, adab, 0, 6, "mod", dm)
    for v in (1, 4):
        p.op("dve", lambda e, v=v: e.tensor_scalar(out=dm[:, v * 8:(v + 1) * 8, :], in0=dm[:, v * 8:(v + 1) * 8, :], scalar1=1.0, scalar2=None, op0=ALU.add), ["mod"], ["mod"])
    for v in (2, 5):
        p.op("dve", lambda e, v=v: e.tensor_scalar(out=dm[:, v * 8:(v + 1) * 8, :], in0=dm[:, v * 8:(v + 1) * 8, :], scalar1=1.0 / ALPHA, scalar2=None, op0=ALU.mult), ["mod"], ["mod"])
    col = lambda v, w: (lambda c: dm[:, v * 8 + c, w:w + 1])
    lcol = lambda i: (lambda c: ln[:, i, c:c + 1])

    with p.scope():
        cx.alloc_work(8 * 128, 256)
        T = 256
        kT = p.sb("kT", [128, 8, 2560], BF16)
        vtm = p.sb("vtm", [128, 20, D], BF16)
        kcT = p.sb("kcT", [128, 8, 256], BF16)
        vctm = p.sb("vctm", [128, 2, D], BF16)
        tab = p.sb("tab", [128, 15, 16, 64], BF16)
        rm = p.sb("rm", [128, 8, 6, 4])
        for dd in range(15):
            io = cx.io[dd % 2]
            p.dma(io[:], tabd[:, dd * 1024:(dd + 1) * 1024], writes=[("io", dd % 2)])
            p.op("pool", lambda e, dd=dd, io=io: e.tensor_copy(out=tab[:, dd, :, :].rearrange("p a b -> p (a b)"), in_=io[:]), [("io", dd % 2)], ["tab"])
        p.dma(rm[:].rearrange("p a b c -> p (a b c)"), rmd, writes=["rm"])
        xt = p.sb("xt", [128, 8, T])
        hT = p.sb("hT", [128, 8, T], BF16)
        vT = p.sb("vT", [128, 8, T], BF16)
        idb = p.sb("idb", [128, 128], BF16)
        p.op("dve", lambda e: e.tensor_copy(out=idb[:], in_=cx.ident[:]), ["ident"], ["idb"])
        pbv = p.ps_alias = None

        def kv_pass(rows_ap, ntok, w_idx, kdst, vdst, t_base, ch_base):
            for t0 in range(0, ntok, T):
                cx.load_T(rows_ap[t0:t0 + T, :], T, xt[:], "xt")
                cx.modulate(T, xt[:], "xt", hT[:], "hT", col(1, w_idx), col(0, w_idx), ["mod"])
                for m in range(8):
                    b, bk = cx.wblock(win, 8, D + m * 128)
                    ps = cx.pb[m % 2]; pk = "pb%d" % (m % 2)
                    for k in range(8):
                        p.op("pe", lambda e, k=k, b=b, ps=ps: e.matmul(ps[:, :T], lhsT=b[:, k, :], rhs=hT[:, k, :], start=(k == 0), stop=(k == 7)), [bk, ("hT", k)], [pk])
                    p.op("act", lambda e, m=m, ps=ps: e.copy(out=kdst[:, m, t_base + t0:t_base + t0 + T], in_=ps[:, :T]), [pk], [("kT", m)])
                for m in range(8):
                    b, bk = cx.wblock(win, 8, 2 * D + m * 128)
                    ps = cx.pb[m % 2]; pk = "pb%d" % (m % 2)
                    for k in range(8):
                        p.op("pe", lambda e, k=k, b=b, ps=ps: e.matmul(ps[:, :T], lhsT=b[:, k, :], rhs=hT[:, k, :], start=(k == 0), stop=(k == 7)), [bk, ("hT", k)], [pk])
                    p.op("act", lambda e, m=m, ps=ps: e.copy(out=vT[:, m, :], in_=ps[:, :T]), [pk], [("vT", m)])
                for s in range(T // 128):
                    psb = cx.pb[2 + s % 2][:].bitcast(BF16)
                    pk = "pb%d" % (2 + s % 2)
                    for m in range(8):
                        p.op("pe", lambda e, m=m, s=s, psb=psb: e.transpose(psb[:, m * 128:(m + 1) * 128], vT[:, m, s * 128:(s + 1) * 128], idb[:]), [("vT", m), "idb"], [pk])
                    ch = ch_base + (t0 // 128) + s
                    p.op("dve", lambda e, ch=ch, psb=psb: e.tensor_copy(out=vdst[:, ch, :], in_=psb), [pk], [("vtm", ch)])

        kv_pass(x1h, 2560, 0, kT, vtm, 0, 0)
        kv_pass(xc1, 256, 1, kcT, vctm, 0, 0)
        p.barrier()

        qT = p.sb("qT", [128, 8, T], BF16)
        oT = p.sb("oT", [128, 8, T], BF16)
        pT = [p.sb("pT%d" % i, [128, T], BF16) for i in range(8)]
        stmp = [p.sb("stmp%d" % i, [128, T]) for i in range(2)]
        rden = p.sb("rden", [128, T])
        slot = [0]
        for b in range(8):
            q0 = 256 + b * 256
            cx.load_T(x1h[q0:q0 + T, :], T, xt[:], "xt")
            cx.modulate(T, xt[:], "xt", hT[:], "hT", col(1, 0), col(0, 0), ["mod"])
            for m in range(8):
                bw, bk = cx.wblock(win, 8, m * 128)
                ps = cx.pb[6 + m % 2]; pk = "pb%d" % (6 + m % 2)
                for k in range(8):
                    p.op("pe", lambda e, k=k, bw=bw, ps=ps: e.matmul(ps[:, :T], lhsT=bw[:, k, :], rhs=hT[:, k, :], start=(k == 0), stop=(k == 7)), [bk, ("hT", k)], [pk])
                p.op("act", lambda e, m=m, ps=ps: e.mul(out=qT[:, m, :], in_=ps[:, :T], mul=0.125), [pk], [("qT", m)])
            for c in range(8):
                for par in range(2):
                    h = 2 * c + par
                    pr = slice(par * 64, par * 64 + 64)
                    for j in range(8):
                        sl = slot[0] % 8
                        slot[0] += 1
                        sps = cx.pb[sl // 2][:, (sl % 2) * 256:(sl % 2) * 256 + 256]
                        sk = ("S", sl)
                        if j < 6:
                            u0 = b * 256 + j * 128
                            lhs = kT[pr, c, u0:u0 + 128]
                            kk_ = ("kT", c)
                        else:
                            lhs = kcT[pr, c, (j - 6) * 128:(j - 5) * 128]
                            kk_ = ("kT", c)
                        p.op("pe", lambda e, lhs=lhs, sps=sps, pr=pr, c=c: e.matmul(sps, lhsT=lhs, rhs=qT[pr, c, :], start=True, stop=True, tile_position=(pr.start, 0)), [kk_, ("qT", c)], [sk])
                        pt = pT[sl]
                        if j < 6:
                            tmp = stmp[j % 2]
                            tk = ("stmp", j % 2)
                            d0 = 11 - 2 * j
                            p.op("dve", lambda e, tmp=tmp, sps=sps, d0=d0, h=h: e.tensor_tensor(out=tmp[:].rearrange("p (a b) -> p a b", b=64), in0=sps.rearrange("p (a b) -> p a b", b=64),
                                                                                   in1=tab[:, d0:d0 + 4, h, :], op=ALU.add), [sk, "tab"], [tk])
                            p.op("pool", lambda e, tmp=tmp, b=b, j=j: e.tensor_tensor(out=tmp[:].rearrange("p (a b) -> p a b", b=64), in0=tmp[:].rearrange("p (a b) -> p a b", b=64),
                                                                                    in1=rm[:, b, j, :].unsqueeze(2).to_broadcast([128, 4, 64]), op=ALU.add), [tk, "rm"], [tk])
                            p.op("act", lambda e, tmp=tmp, pt=pt: e.activation(out=pt[:], in_=tmp[:], func=AF.Exp), [tk], [("pT", sl)])
                            vl = vtm[:, 2 * b + j, h * 64:(h + 1) * 64]
                            vk = ("vtm", 2 * b + j)
                        else:
                            p.op("act", lambda e, sps=sps, pt=pt: e.activation(out=pt[:], in_=sps, func=AF.Exp), [sk], [("pT", sl)])
                            vl = vctm[:, j - 6, h * 64:(h + 1) * 64]
                            vk = ("vtm", j - 6)
                        nps = cx.pb[4][pr, :T]
                        dps = cx.pb[5][pr, :T]
                        p.op("pe", lambda e, vl=vl, pt=pt, nps=nps, j=j, par=par: e.matmul(nps, lhsT=vl, rhs=pt[:], start=(j == 0), stop=(j == 7), tile_position=(0, par * 64)), [vk, ("pT", sl)], [("num", par)])
                        p.op("pe", lambda e, pt=pt, dps=dps, j=j, par=par: e.matmul(dps, lhsT=cx.onesb[:], rhs=pt[:], start=(j == 0), stop=(j == 7), tile_position=(0, par * 64)), ["onesb", ("pT", sl)], [("den", par)])
                p.op("dve", lambda e: e.reciprocal(out=rden[:], in_=cx.pb[5][:, :T]), [("den", 0), ("den", 1)], ["rden"])
                p.op("dve", lambda e, c=c: e.tensor_tensor(out=oT[:, c, :], in0=cx.pb[4][:, :T], in1=rden[:], op=ALU.mult), [("num", 0), ("num", 1), "rden"], [("oT", c)])
            def emit_wo(c, ps, pk):
                bw, bk = cx.wblock(wout, 8, c * 128)
                for k in range(8):
                    p.op("pe", lambda e, k=k, bw=bw: e.matmul(ps, lhsT=bw[:, k, :], rhs=oT[:, k, :], start=(k == 0), stop=(k == 7)),
                         [bk, ("oT", k)], [pk, ("num", 0), ("num", 1), ("den", 0), ("den", 1)])
            cx.tail(T, emit_wo, xt[:], "xt", col(2, 0), lcol(0), lcol(1), modkeys=["mod"])
            p.dma(xmid[:, :, b * 256:(b + 1) * 256], xt[:], reads=[("xt", c) for c in range(8)], writes=["xmid"])
            p.barrier()

    with p.scope():
        T = 512
        cx.alloc_work(NFC * 128, T)
        xt = p.sb("xt3", [128, 8, T])
        hT = p.sb("hT3", [128, 8, T], BF16)
        actT = p.sb("actT", [128, NFC, T], BF16)
        for t in range(4):
            p.dma(xt[:], xmid[:, :, t * T:(t + 1) * T], reads=["xmid"], writes=[("xt", c) for c in range(8)])
            cx.modulate(T, xt[:], "xt", hT[:], "hT", col(4, 0), col(3, 0), ["mod"])
            cx.ffn_up(T, hT[:], "hT", w1, w3, actT)
            cx.tail(T, cx.ffn_down_emit(T, w2, actT), xt[:], "xt", col(5, 0), lcol(2), lcol(3), modkeys=["mod"])
            cx.store_T(xt[:], "xt", T, out[t * T:(t + 1) * T, :])
    p.finish()
    return nc


def colmajor(v):
    v = np.asarray(v, np.float32)
    return np.ascontiguousarray(v.reshape(-1, 128).T)


def host_C_consts(rpb):
    rpb = np.asarray(rpb, np.float32)
    tab = np.full((2, 64, 15, 16, 64), NEG, np.float32)
    cq = np.arange(64)
    sc = np.clip(cq - 8, 0, 48)
    cp = np.arange(64)[:, None]
    valid = (cp >= sc[None, :]) & (cp < sc[None, :] + 16)
    coff = np.clip(cp - cq[None, :] + 15, 0, 30)
    for ipar in range(2):
        for dprime in range(15):
            delta = 7 - dprime + ipar
            ro = delta + 7
            if ro < 0 or ro > 14:
                continue
            for h in range(16):
                g = rpb[h, ro][coff]
                tab[ipar, :, dprime, h, :] = np.where(valid, g, np.float32(NEG))
    return tab.reshape(128, 15 * 16 * 64)


def host_C_rowmask(core):
    rm = np.full((2, 64, 8, 6, 4), NEG, np.float32)
    R0 = core * 32
    for b in range(8):
        for j in range(6):
            for ipar in range(2):
                krow = R0 + 4 * b - 4 + 2 * j + ipar
                for rq in range(4):
                    qrow = R0 + 4 * b + rq
                    sr = min(max(qrow - 4, 0), 248)
                    if 0 <= krow < 256 and sr <= krow < sr + 8:
                        rm[ipar, :, b, j, rq] = 0.0
    return rm.reshape(128, 8 * 6 * 4)


def run_C(x1, xc1, c, c_ctx, ada_w1, ada_b1, ln_g1, ln_b1, w1, w3, w2, win, rpb, wout):
    from concourse.bass_utils import run_bass_kernel_spmd
    nc = build_C()
    x1 = np.asarray(x1, np.float32).reshape(16384, D)
    xp = np.concatenate([np.zeros((256, D), np.float32), x1, np.zeros((256, D), np.float32)], 0)
    cc = np.ascontiguousarray(np.stack([colmajor(c.reshape(-1)), colmajor(c_ctx.reshape(-1))], -1))
    lnc = np.ascontiguousarray(np.stack([colmajor(ln_g1[0]), colmajor(ln_b1[0]), colmajor(ln_g1[1]), colmajor(ln_b1[1])], 1))
    tab = host_C_consts(rpb)
    shared = {"xc1": np.ascontiguousarray(xc1.reshape(256, D)), "cc": cc, "adaw": np.ascontiguousarray(ada_w1), "adab": colmajor(ada_b1),
              "lnc": lnc, "w1": np.ascontiguousarray(w1), "w3": np.ascontiguousarray(w3), "w2": np.ascontiguousarray(w2),
              "win": np.ascontiguousarray(win), "wout": np.ascontiguousarray(wout), "tab": tab, "ident": np.eye(128, dtype=np.float32)}
    in_maps = []
    for k in range(8):
        m = dict(shared)
        m["x1h"] = np.ascontiguousarray(xp[k * 2048:k * 2048 + 2560])
        m["rowmask"] = host_C_rowmask(k)
        in_maps.append(m)
    res = run_bass_kernel_spmd(nc, in_maps, core_ids=list(range(8)))
    return np.concatenate([r["out"] for r in res.results], 0).reshape(1, 16384, D)


class Ops:
    def __init__(self, p):
        self.p = p

    def tt(self, eng, out, in0, in1, op, R, W):
        return self.p.op(eng, lambda e: e.tensor_tensor(out=out, in0=in0, in1=in1, op=op), R, W)

    def ts(self, eng, out, in0, s1, s2, op0, op1, R, W):
        if s2 is None:
            return self.p.op(eng, lambda e: e.tensor_scalar(out=out, in0=in0, scalar1=s1, scalar2=None, op0=op0), R, W)
        return self.p.op(eng, lambda e: e.tensor_scalar(out=out, in0=in0, scalar1=s1, scalar2=s2, op0=op0, op1=op1), R, W)

    def stt(self, eng, out, in0, scalar, in1, op0, op1, R, W):
        return self.p.op(eng, lambda e: e.scalar_tensor_tensor(out=out, in0=in0, scalar=scalar, in1=in1, op0=op0, op1=op1), R, W)

    def act(self, out, in_, func, R, W, scale=None, bias=None):
        kw = {}
        if scale is not None:
            kw["scale"] = scale
        if bias is not None:
            kw["bias"] = bias
        return self.p.op("act", lambda e: e.activation(out=out, in_=in_, func=func, **kw), R, W)

    def cp(self, eng, out, in_, R, W):
        if eng == "act":
            return self.p.op("act", lambda e: e.copy(out=out, in_=in_), R, W)
        return self.p.op(eng, lambda e: e.tensor_copy(out=out, in_=in_), R, W)

    def mm(self, out, lhsT, rhs, R, W, start=True, stop=True, tp=None):
        if tp is None:
            return self.p.op("pe", lambda e: e.matmul(out, lhsT=lhsT, rhs=rhs, start=start, stop=stop), R, W)
        return self.p.op("pe", lambda e: e.matmul(out, lhsT=lhsT, rhs=rhs, start=start, stop=stop, tile_position=tp), R, W)

    def tr(self, out, in_, ident, R, W):
        return self.p.op("pe", lambda e: e.transpose(out, in_, ident), R, W)


NTA = 272
C_LWS = -0.6065306597126334


def build_A():
    nc = bass.Bass("TRN2", target_bir_lowering=False)
    dt = lambda n, s, k="ExternalInput", d=F32: nc.dram_tensor(n, list(s), d, kind=k).ap()
    xh = dt("xh", [9 * NTA, D]); cmd = dt("colmask", [128, 9 * NTA]); invd = dt("invcnt", [128, 9 * 2 * 256])
    cc = dt("cc", [128, 8, 2]); adaw = dt("adaw", [D, 6 * D]); adab = dt("adab", [128, 48])
    win = dt("win", [D, 2944]); pcd = dt("pcols", [128, 86])
    w2d = dt("w2t", [128, 768]); a2d = dt("a2t", [128, 768]); g2d = dt("g2", [128, 768])
    pwd = dt("poolw", [128, 256]); bod = dt("bones", [128, 128]); mkd = dt("masks", [64, 4 * 512]); identd = dt("ident", [128, 128])
    spd = dt("sp", [2, 6, 36, 128, 256], "ExternalOutput")
    goutd = dt("gout", [128, 6, 2304], "ExternalOutput"); bond = dt("bonus", [128, 6, 2304], "ExternalOutput")
    boutd = dt("bout", [128, 2, 2304], "ExternalOutput"); totd = dt("tot", [2, 6, 128, 128], "ExternalOutput")
    p = Prog(nc)
    o = Ops(p)
    cx = Cx(p, nc, identd, 272)
    pb = cx.pb
    PB = lambda i: "pb%d" % i
    dm = p.sb("mod_dm", [128, 16, 2])
    pc = p.sb("pc", [128, 86]); w2t = p.sb("w2t", [128, 768]); a2t = p.sb("a2t", [128, 768]); g2t = p.sb("g2t", [128, 768])
    pw = p.sb("pw", [128, 2, 128]); bones = p.sb("bones", [128, 128]); mk = p.sb("mk", [64, 4, 512])
    cm = p.sb("cm", [128, 9, NTA]); AB = p.sb("AB", [128, 12, 128]); id2 = p.sb("id2", [128, 64])
    for t_, d_, k_ in ((pc, pcd, "pc"), (w2t, w2d, "w2t"), (a2t, a2d, "a2t"), (g2t, g2d, "g2t"), (bones, bod, "bones")):
        p.dma(t_[:], d_, writes=[k_])
    p.dma(pw[:].rearrange("p a b -> p (a b)"), pwd, writes=["pw"])
    p.dma(mk[:].rearrange("p a b -> p (a b)"), mkd, writes=["mk"])
    p.dma(cm[:].rearrange("p a b -> p (a b)"), cmd, writes=["cm"])
    p.dma(id2[0:64, :], identd[0:64, 0:64], writes=["id2"])
    p.dma(id2[64:128, :], identd[0:64, 0:64], writes=["id2"])
    with p.scope():
        cx.alloc_work(8 * 128, 16)
        cx.mod_cols(cc, adaw, adab, 0, 2, "mod", dm)
    o.ts("dve", dm[:, 8:16, :], dm[:, 8:16, :], 1.0, None, ALU.add, None, ["mod"], ["mod"])
    col = lambda v, w: (lambda c: dm[:, v * 8 + c, w:w + 1])
    MU0, MU1, W0, A0, KK_, KA_, RK_, PS_, A0C = 0, 21, 42, 54, 66, 72, 78, 84, None
    a0c = p.sb("a0c", [128, 21])
    o.tt("dve", a0c[:], pc[:, 0:21], pc[:, 21:42], ALU.add, ["pc"], ["a0c"])
    o.ts("dve", a0c[:], a0c[:], -1.0, 1.0, ALU.mult, ALU.add, ["a0c"], ["a0c"])
    pcol = lambda base, i: pc[:, base + i:base + i + 1]
    for i in range(12):
        o.cp("pool", AB[:, i, 0:64], id2[:], ["id2"], [("AB", i)])
        p.op("pool", lambda e, i=i: e.memset(AB[:, i, 64:128], 0.0), [], [("AB", i)])
    MSK = {"Us": 0, "Ui": 1, "Ls": 2, "Li": 3}

    with p.scope():
        cx.alloc_work(8 * 128, 16)
        xT = p.sb("xT", [128, 8, NTA]); hT = p.sb("hT", [128, 8, NTA], BF16)
        P = p.sb("P", [128, 23, NTA]); Q = p.sb("Q", [128, 21, 256])
        inv = p.sb("inv", [128, 2, 256]); pa = p.sb("pa", [128, NTA]); pbf = p.sb("pbf", [128, NTA]); pooled = p.sb("pooled", [128, 2, 256])
        bst = p.sb("bst", [128, 2, 256])
        tw = p.sb("tw", [128, 256]); sg = p.sb("sg", [128, 256]); gst = p.sb("gst", [128, 256]); bon = p.sb("bon", [128, 256])
        kk = p.sb("kk", [128, 256]); t1 = p.sb("t1", [128, 256]); t2 = p.sb("t2", [128, 256])
        lw = [p.sb("lw%d" % d, [128, 256]) for d in range(2)]; aa = [p.sb("aa%d" % d, [128, 256]) for d in range(2)]
        kd = [p.sb("kd%d" % d, [128, 256]) for d in range(2)]; bb = [p.sb("bb%d" % d, [128, 256]) for d in range(2)]
        cl = p.sb("cl", [128, 256]); cle = p.sb("cle", [128, 256]); E = p.sb("E", [128, 256]); totc = p.sb("totc", [128, 4]); clc = p.sb("clc", [128, 4]); PC = p.sb("PC", [128, 4])
        kkt = p.sb("kkt", [128, 256]); rt = p.sb("rt", [128, 256]); bt = p.sb("bt", [128, 256]); kt = p.sb("kt", [128, 256]); bh = p.sb("bh", [128, 256]); kh = p.sb("kh", [128, 256])
        Vtm = p.sb("Vtm", [64, 4, 128]); BHtm = p.sb("BHtm", [64, 4, 128]); KHtm = p.sb("KHtm", [64, 4, 128]); X = p.sb("X", [64, 4, 2, 128])
        Lb = [p.sb("Lb%d" % i, [64, 512]) for i in range(2)]; LTb = [p.sb("LTb%d" % i, [64, 512]) for i in range(2)]
        AkkT = p.sb("AkkT", [64, 512]); AbT = p.sb("AbT", [64, 512]); AkT = p.sb("AkT", [64, 512])
        ost = p.sb("ost", [128, 4, 4, 64]); Mb = p.sb("Mb", [128, 4, 64]); idpc = p.sb("idpc", [128, 4, 64])
        v4 = lambda t: t[:].rearrange("p (q n) -> p q n", n=64)

        def stage_dc(d, c, ti, r, v):
            fwd = (d == 0)
            for q in range(4):
                p.op("dve", lambda e, q=q: e.tensor_tensor_scan(out=cl[:, q * 64:(q + 1) * 64], data0=cx.ones[:, 0:64], data1=lw[d][:, q * 64:(q + 1) * 64], initial=0.0, op0=ALU.mult, op1=ALU.add),
                     [("lw", d), "ones"], ["cl"])
            if not fwd:
                o.cp("pool", totc[:], v4(cl)[:, :, 63], ["cl"], ["totc"])
                o.tt("pool", cle[:], lw[d][:], cl[:], ALU.subtract, [("lw", d), "cl"], ["cle"])
                o.tt("dve", v4(cl), v4(cle), totc[:].unsqueeze(2).to_broadcast([128, 4, 64]), ALU.add, ["cle", "totc"], ["cl"])
            o.tt("pool", cle[:], cl[:], lw[d][:], ALU.subtract, ["cl", ("lw", d)], ["cle"])
            o.cp("pool", clc[:], v4(cl)[:, :, 63 if fwd else 0], ["cl"], ["clc"])
            o.act(E[:], cle[:], AF.Exp, ["cle"], ["E"])
            o.tt("dve", kkt[:], kk[:], E[:], ALU.mult, ["kk", "E"], ["kkt"])
            o.act(E[:], cl[:], AF.Exp, ["cl"], ["E"])
            o.tt("dve", rt[:], r, E[:], ALU.mult, ["Q", "E"], ["rt"])
            o.act(E[:], cl[:], AF.Exp, ["cl"], ["E"], scale=-1.0)
            o.tt("dve", bt[:], bb[d][:], E[:], ALU.mult, [("bb", d), "E"], ["bt"])
            o.tt("pool", kt[:], kd[d][:], E[:], ALU.mult, [("kd", d), "E"], ["kt"])
            o.tt("dve", v4(cle), clc[:].unsqueeze(2).to_broadcast([128, 4, 64]), v4(cl), ALU.subtract, ["clc", "cl"], ["cle"])
            o.act(E[:], cle[:], AF.Exp, ["cle"], ["E"])
            o.tt("dve", bh[:], bb[d][:], E[:], ALU.mult, [("bb", d), "E"], ["bh"])
            o.tt("pool", kh[:], kd[d][:], E[:], ALU.mult, [("kd", d), "E"], ["kh"])
            o.act(PC[:], clc[:], AF.Exp, ["clc"], ["PC"])
            for q in range(4):
                o.tr(pb[0][0:64, q * 128:(q + 1) * 128], kkt[:, q * 64:(q + 1) * 64], cx.ident[:], ["kkt", "ident"], [PB(0)])
                o.tr(pb[1][0:64, q * 128:(q + 1) * 128], bh[:, q * 64:(q + 1) * 64], cx.ident[:], ["bh", "ident"], [PB(1)])
                o.tr(pb[2][0:64, q * 128:(q + 1) * 128], kh[:, q * 64:(q + 1) * 64], cx.ident[:], ["kh", "ident"], [PB(2)])
            o.cp("act", X[:, :, :, 0:64], pb[0][0:64, :].rearrange("p (q a n) -> p q a n", a=2, n=64), [PB(0)], ["X"])
            o.cp("dve", BHtm[:], pb[1][0:64, :].rearrange("p (q n) -> p q n", n=128), [PB(1)], ["BHtm"])
            o.cp("pool" if False else "act", KHtm[:], pb[2][0:64, :].rearrange("p (q n) -> p q n", n=128), [PB(2)], ["KHtm"])
            for q in range(4):
                for par in range(2):
                    pr = slice(par * 64, par * 64 + 64)
                    qs = slice(q * 64, q * 64 + 64)
                    cs = slice((q * 2 + par) * 64, (q * 2 + par) * 64 + 64)
                    tp = (par * 64, 0)
                    o.mm(pb[0][0:64, cs], bt[pr, qs], kkt[pr, qs], ["bt", "kkt"], [PB(0)], tp=tp)
                    o.mm(pb[1][0:64, cs], kkt[pr, qs], bt[pr, qs], ["bt", "kkt"], [PB(1)], tp=tp)
                    o.mm(pb[2][0:64, cs], kt[pr, qs], kkt[pr, qs], ["kt", "kkt"], [PB(2)], tp=tp)
                    o.mm(pb[4][0:64, cs], bt[pr, qs], rt[pr, qs], ["bt", "rt"], [PB(4)], tp=tp)
                    o.mm(pb[5][0:64, cs], kt[pr, qs], rt[pr, qs], ["kt", "rt"], [PB(5)], tp=tp)
            mS_st = mk[:, MSK["Us" if fwd else "Ls"], :]
            mS_ts = mk[:, MSK["Ls" if fwd else "Us"], :]
            mI_st = mk[:, MSK["Ui" if fwd else "Li"], :]
            o.tt("dve", LTb[0][:], pb[0][0:64, :], mS_st, ALU.mult, [PB(0), "mk"], [("LT", 0)])
            o.tt("pool", Lb[0][:], pb[1][0:64, :], mS_ts, ALU.mult, [PB(1), "mk"], [("L", 0)]) if False else o.tt("dve", Lb[0][:], pb[1][0:64, :], mS_ts, ALU.mult, [PB(1), "mk"], [("L", 0)])
            o.tt("dve", AkkT[:], pb[2][0:64, :], mS_st, ALU.mult, [PB(2), "mk"], ["AkkT"])
            o.tt("dve", AbT[:], pb[4][0:64, :], mI_st, ALU.mult, [PB(4), "mk"], ["AbT"])
            o.tt("dve", AkT[:], pb[5][0:64, :], mI_st, ALU.mult, [PB(5), "mk"], ["AkT"])
            for q in range(4):
                for par in range(2):
                    cs = slice((q * 2 + par) * 64, (q * 2 + par) * 64 + 64)
                    o.mm(pb[6][0:64, cs], AkkT[:, cs], Vtm[:, q, par * 64:(par + 1) * 64], ["AkkT", "Vtm"], [PB(6)])
            o.ts("dve", X[:, :, :, 64:128], pb[6][0:64, :].rearrange("p (q a n) -> p q a n", a=2, n=64), -1.0, None, ALU.mult, None, [PB(6)], ["X"])

            def apply(LTcur, ltkey, sub):
                for idx in range(8):
                    q, par = idx // 2, idx % 2
                    bank = 6 + idx // 4
                    o.mm(pb[bank][0:64, (idx % 4) * 128:(idx % 4 + 1) * 128], LTcur[:, idx * 64:(idx + 1) * 64], X[:, q, par, :], [ltkey, "X"], [PB(bank)])
                for hf in range(2):
                    o.tt("dve", X[:, 2 * hf:2 * hf + 2, :, :], X[:, 2 * hf:2 * hf + 2, :, :], pb[6 + hf][0:64, :].rearrange("p (q a n) -> p q a n", a=2, n=128),
                         ALU.subtract if sub else ALU.add, ["X", PB(6 + hf)], ["X"])

            apply(LTb[0], ("LT", 0), True)
            cur = 0
            for k in range(5):
                nxt = cur ^ 1
                for idx in range(8):
                    cs = slice(idx * 64, idx * 64 + 64)
                    o.mm(pb[0][0:64, cs], LTb[cur][:, cs], Lb[cur][:, cs], [("LT", cur), ("L", cur)], [PB(0)])
                    o.mm(pb[1][0:64, cs], Lb[cur][:, cs], LTb[cur][:, cs], [("LT", cur), ("L", cur)], [PB(1)])
                o.cp("act", Lb[nxt][:], pb[0][0:64, :], [PB(0)], [("L", nxt)])
                o.cp("dve", LTb[nxt][:], pb[1][0:64, :], [PB(1)], [("LT", nxt)])
                apply(LTb[nxt], ("LT", nxt), False)
                cur = nxt
            for q in range(4):
                for par in range(2):
                    pr = slice(par * 64, par * 64 + 64)
                    qs = slice(q * 64, q * 64 + 64)
                    cs = slice((q * 2 + par) * 64, (q * 2 + par) * 64 + 64)
                    tp = (0, par * 64)
                    Wt = X[:, q, par, 0:64]
                    Ut = X[:, q, par, 64:128]
                    BHq = BHtm[:, q, par * 64:(par + 1) * 64]
                    KHq = KHtm[:, q, par * 64:(par + 1) * 64]
                    Vq = Vtm[:, q, par * 64:(par + 1) * 64]
                    o.mm(pb[2][pr, qs], Wt, BHq, ["X", "BHtm"], [PB(2)], tp=tp)
                    o.mm(pb[2][pr, 256 + q * 64:256 + q * 64 + 64], BHq, Ut, ["X", "BHtm"], [PB(2)], start=True, stop=False, tp=tp)
                    o.mm(pb[2][pr, 256 + q * 64:256 + q * 64 + 64], KHq, Vq, ["KHtm", "Vtm"], [PB(2)], start=False, stop=True, tp=tp)
                    o.mm(pb[4][pr, qs], Wt, AbT[:, cs], ["X", "AbT"], [PB(4)], tp=tp)
                    o.mm(pb[4][pr, 256 + q * 64:256 + q * 64 + 64], Ut, AbT[:, cs], ["X", "AbT"], [PB(4)], start=True, stop=False, tp=tp)
                    o.mm(pb[4][pr, 256 + q * 64:256 + q * 64 + 64], Vq, AkT[:, cs], ["Vtm", "AkT"], [PB(4)], start=False, stop=True, tp=tp)
                    if not fwd:
                        o.mm(pb[5][pr, qs], BHq, Wt, ["X", "BHtm"], [PB(5)], tp=tp)
            o.tt("pool", idpc[:], id2[:].unsqueeze(1).to_broadcast([128, 4, 64]), PC[:].unsqueeze(2).to_broadcast([128, 4, 64]), ALU.mult, ["id2", "PC"], ["idpc"])
            o.tt("dve", ost[:, :, 0, :], idpc[:], pb[2][:, 0:256].rearrange("p (q n) -> p q n", n=64), ALU.subtract, ["idpc", PB(2)], ["ost"])
            o.cp("act", ost[:, :, 1, :], pb[2][:, 256:512].rearrange("p (q n) -> p q n", n=64), [PB(2)], ["ost"])
            o.tt("dve", ost[:, :, 2, :], v4(rt), pb[4][:, 0:256].rearrange("p (q n) -> p q n", n=64), ALU.subtract, ["rt", PB(4)], ["ost"])
            o.cp("act", ost[:, :, 3, :], pb[4][:, 256:512].rearrange("p (q n) -> p q n", n=64), [PB(4)], ["ost"])
            ch0 = ti * 4
            p.dma(spd[d, c, ch0:ch0 + 4].rearrange("q p x -> p q x"), ost[:].rearrange("p q k n -> p q (k n)"), reads=["ost"])
            if ti >= 8:
                return
            if not fwd:
                o.tt("dve", Mb[:], idpc[:], pb[5][:, 0:256].rearrange("p (q n) -> p q n", n=64), ALU.subtract, ["idpc", PB(5)], ["Mb"])
            ab = AB[:, d * 6 + c, :]
            abk = ("AB", d * 6 + c)
            for q in range(4):
                for par in range(2):
                    pr = slice(par * 64, par * 64 + 64)
                    tp = (par * 64, par * 64)
                    if fwd:
                        o.mm(pb[3][pr, 0:128], ost[pr, q, 0, :], ab[pr, :], ["ost", abk], [PB(3)], tp=tp)
                    else:
                        o.mm(pb[3][pr, 0:64], Mb[pr, q, :], ab[pr, 0:64], ["Mb", abk], [PB(3)], tp=tp)
                        o.mm(pb[3][pr, 64:128], ab[pr, 0:64], ost[pr, q, 1, :], ["ost", abk], [PB(3)], tp=tp)
                o.cp("act", ab[:, 0:64], pb[3][:, 0:64], [PB(3)], [abk])
                if fwd:
                    o.tt("dve", ab[:, 64:128], pb[3][:, 64:128], ost[:, q, 1, :], ALU.add, [PB(3), "ost"], [abk])
                else:
                    o.tt("dve", ab[:, 64:128], pb[3][:, 64:128], ab[:, 64:128], ALU.add, [PB(3), abk], [abk])

        for ti in range(9):
            w = 0 if ti < 8 else 1
            tok0 = ti * 256
            cx.load_T(xh[ti * NTA:(ti + 1) * NTA, :], NTA, xT[:], "xT")
            cx.modulate(NTA, xT[:], "xT", hT[:], "hT", col(1, w), col(0, w), ["mod"])
            p.dma(inv[:].rearrange("p a b -> p (a b)"), invd[:, ti * 512:(ti + 1) * 512], writes=["inv"])
            for m in range(23):
                bw, bk = cx.wblock(win, 8, m * 128)
                ps = pb[m % 2]
                for k in range(8):
                    o.mm(ps[:, :NTA], bw[:, k, :], hT[:, k, :], [bk, ("hT", k)], [PB(m % 2)], start=(k == 0), stop=(k == 7))
                o.tt("dve", P[:, m, :], ps[:, :NTA], cm[:, ti, :], ALU.mult, [PB(m % 2), "cm"], [("P", m)])
            for m in range(21):
                o.act(Q[:, m, :], P[:, m, 8:264], AF.Identity, [("P", m), "a0c"], ["Q"], scale=a0c[:, m:m + 1])
                o.stt("dve", Q[:, m, :], P[:, m, 7:263], pcol(MU0, m), Q[:, m, :], ALU.mult, ALU.add, [("P", m), "pc", "Q"], ["Q"])
                o.stt("dve", Q[:, m, :], P[:, m, 9:265], pcol(MU1, m), Q[:, m, :], ALU.mult, ALU.add, [("P", m), "pc", "Q"], ["Q"])
            for gi, (m, par, wdw) in enumerate(((21, 0, 2), (21, 1, 4), (22, 0, 8), (22, 1, 16))):
                pr = slice(par * 64, par * 64 + 64)
                mi = m - 21
                eng = "pool" if gi % 2 else "dve"
                src = P[pr, m, :]
                cur, curk = src, ("P", m)
                bufs = [(pa[pr, :], ("pa", par)), (pbf[pr, :], ("pbf", par))]
                n, k, i = NTA, 1, 0
                while k < wdw:
                    n2 = n - k
                    o.tt(eng, bufs[i][0][:, 0:n2], cur[:, 0:n2], cur[:, k:k + n2], ALU.add, [curk], [bufs[i][1]])
                    cur, curk = bufs[i]
                    i ^= 1
                    n = n2
                    k *= 2
                x0 = 8 - wdw // 2
                o.tt(eng, pooled[pr, mi, :], cur[:, x0:x0 + 256], inv[pr, mi, :], ALU.mult, [curk, "inv"], [("pooled", gi)])
                o.tt(eng, pooled[pr, mi, :], pooled[pr, mi, :], src[:, 8:264], ALU.subtract, [("pooled", gi), ("P", m)], [("pooled", gi)])
            for mi in range(2):
                o.mm(pb[2 + mi][:, 0:256], pw[:, mi, :], pooled[:, mi, :], ["pw", ("pooled", 2 * mi), ("pooled", 2 * mi + 1)], [PB(2 + mi)])
                o.ts("dve", bst[:, mi, :], pb[2 + mi][:, 0:256], pcol(PS_, mi), None, ALU.mult, None, [PB(2 + mi), "pc"], ["bst"])
            p.dma(boutd[:, :, tok0:tok0 + 256], bst[:], reads=["bst"])
            o.act(tw[:], Q[:, 18, :], AF.Tanh, ["Q"], ["tw"])
            o.act(sg[:], Q[:, 20, :], AF.Sigmoid, ["Q"], ["sg"])
            for c in range(6):
                r = Q[:, c, :]
                k_ = Q[:, 6 + c, :]
                v = Q[:, 12 + c, :]
                o.mm(pb[2][:, 0:256], g2t[:, c * 128:(c + 1) * 128], sg[:], ["g2t", "sg"], [PB(2)])
                o.cp("act", gst[:], pb[2][:, 0:256], [PB(2)], ["gst"])
                p.dma(goutd[:, c, tok0:tok0 + 256], gst[:], reads=["gst"])
                o.ts("dve", t1[:], k_, pcol(KK_, c), None, ALU.mult, None, ["Q", "pc"], ["t1"])
                o.tt("pool", t2[:], t1[:], t1[:], ALU.mult, ["t1"], ["t2"])
                o.mm(pb[3][:, 0:256], bones[:], t2[:], ["bones", "t2"], [PB(3)])
                o.act(t2[:], pb[3][:, 0:256], AF.Sqrt, [PB(3)], ["t2"])
                o.ts("dve", t2[:], t2[:], 1e-12, None, ALU.max, None, ["t2"], ["t2"])
                p.op("dve", lambda e: e.reciprocal(out=t2[:], in_=t2[:]), ["t2"], ["t2"])
                o.tt("dve", kk[:], t1[:], t2[:], ALU.mult, ["t1", "t2"], ["kk"])
                for d in range(2):
                    dr = slice(d * 64, d * 64 + 64)
                    o.mm(pb[2][:, 0:256], w2t[dr, c * 128:(c + 1) * 128], tw[dr, :], ["w2t", "tw"], [PB(2)], tp=(d * 64, 0))
                    o.act(lw[d][:], pb[2][:, 0:256], AF.Sigmoid, [PB(2), "pc"], [("lw", d)], bias=pcol(W0, d * 6 + c))
                    o.ts("pool", lw[d][:], lw[d][:], C_LWS, None, ALU.mult, None, [("lw", d)], [("lw", d)])
                    o.mm(pb[3][:, 0:256], a2t[dr, c * 128:(c + 1) * 128], Q[dr, 19, :], ["a2t", "Q"], [PB(3)], tp=(d * 64, 0))
                    o.act(aa[d][:], pb[3][:, 0:256], AF.Sigmoid, [PB(3), "pc"], [("aa", d)], bias=pcol(A0, d * 6 + c))
                    o.ts("dve", t1[:], aa[d][:], -1.0, pcol(KA_, c), ALU.add, ALU.mult, [("aa", d), "pc"], ["t1"])
                    o.stt("dve", kd[d][:], t1[:], 1.0, k_, ALU.add, ALU.mult, ["t1", "Q"], [("kd", d)])
                    o.tt("pool", bb[d][:], kk[:], aa[d][:], ALU.mult, ["kk", ("aa", d)], [("bb", d)])
                o.tt("dve", t1[:], kd[0][:], kd[1][:], ALU.add, [("kd", 0), ("kd", 1)], ["t1"])
                o.stt("dve", t1[:], t1[:], pcol(RK_, c), r, ALU.mult, ALU.mult, ["t1", "pc", "Q"], ["t1"])
                o.mm(pb[2][:, 0:256], bones[:], t1[:], ["bones", "t1"], [PB(2)])
                o.tt("dve", bon[:], pb[2][:, 0:256], v, ALU.mult, [PB(2), "Q"], ["bon"])
                p.dma(bond[:, c, tok0:tok0 + 256], bon[:], reads=["bon"])
                for q in range(4):
                    o.tr(pb[3][0:64, q * 128:(q + 1) * 128], v[:, q * 64:(q + 1) * 64], cx.ident[:], ["Q", "ident"], [PB(3)])
                o.cp("act", Vtm[:], pb[3][0:64, :].rearrange("p (q n) -> p q n", n=128), [PB(3)], ["Vtm"])
                for d in range(2):
                    stage_dc(d, c, ti, r, v)
        for c in range(6):
            ab = AB[:, c, :]
            for par in range(2):
                pr = slice(par * 64, par * 64 + 64)
                o.mm(pb[3][pr, 0:64], ab[pr, 0:64], cx.ident[pr, pr], [("AB", c), "ident"], [PB(3)], tp=(par * 64, par * 64))
            o.cp("act", ab[:, 0:64], pb[3][:, 0:64], [PB(3)], [("AB", c)])
        for i in range(12):
            p.dma(totd[i // 6, i % 6], AB[:, i, :], reads=[("AB", i)], is_output=True)
    p.finish()
    return nc


def host_A_inputs(x, ctx, c, c_ctx, ada_w0, ada_b0, win, shift_mu, w0, w2, a0, a2, g2, k_k, k_a, r_k, pool_w, pool_scale):
    x = np.asarray(x, np.float32).reshape(16384, D)
    ctx = np.asarray(ctx, np.float32).reshape(256, D)
    xp = np.concatenate([np.zeros((8, D), np.float32), x, np.zeros((8, D), np.float32)], 0)
    cp_ = np.concatenate([np.zeros((8, D), np.float32), ctx, np.zeros((8, D), np.float32)], 0)
    cc = np.ascontiguousarray(np.stack([colmajor(c.reshape(-1)), colmajor(c_ctx.reshape(-1))], -1))
    pcols = np.concatenate([colmajor(shift_mu[0]), colmajor(shift_mu[1]), colmajor(w0.reshape(-1)), colmajor(a0.reshape(-1)),
                            colmajor(k_k), colmajor(k_a), colmajor(r_k.reshape(-1)), colmajor(pool_scale)], 1)
    assert pcols.shape == (128, 86)
    pw = np.zeros((128, 2, 128), np.float32)
    for g in range(4):
        mi, par = g // 2, g % 2
        pw[par * 64:(par + 1) * 64, mi, par * 64:(par + 1) * 64] = pool_w[g]
    bones = np.zeros((128, 128), np.float32)
    bones[:64, :64] = 1.0
    bones[64:, 64:] = 1.0
    pp = np.arange(64)[:, None]
    ff = np.arange(512)[None, :] % 64
    masks = np.stack([(pp < ff), (pp <= ff), (pp > ff), (pp >= ff)], 1).astype(np.float32)
    shared = {"cc": cc, "adaw": np.ascontiguousarray(ada_w0), "adab": colmajor(ada_b0), "win": np.ascontiguousarray(win), "pcols": np.ascontiguousarray(pcols),
              "w2t": np.ascontiguousarray(w2.reshape(128, 768)), "a2t": np.ascontiguousarray(a2.reshape(128, 768)), "g2": np.ascontiguousarray(g2),
              "poolw": pw.reshape(128, 256), "bones": bones, "masks": np.ascontiguousarray(masks.reshape(64, 2048)), "ident": np.eye(128, dtype=np.float32)}
    wins = (2, 4, 8, 16)
    in_maps = []
    for k in range(8):
        xh = np.zeros((9, NTA, D), np.float32)
        cm = np.zeros((128, 9, NTA), np.float32)
        inv = np.ones((128, 9, 2, 256), np.float32)
        for ti in range(9):
            if ti < 8:
                t0 = k * 2048 + ti * 256
                xh[ti] = xp[t0:t0 + NTA]
                L = 16384
            else:
                t0 = 0
                xh[ti] = cp_
                L = 256
            tok = t0 - 8 + np.arange(NTA)
            cm[:, ti, :] = ((tok >= 0) & (tok < L)).astype(np.float32)[None, :]
            t = t0 + np.arange(256)
            for g in range(4):
                mi, par = g // 2, g % 2
                lo = np.clip(t - wins[g] // 2, 0, L)
                hi = np.clip(t + wins[g] // 2, 0, L)
                inv[par * 64:(par + 1) * 64, ti, mi, :] = (np.float32(1.0) / (hi - lo).astype(np.float32))[None, :]
        m = dict(shared)
        m["xh"] = xh.reshape(9 * NTA, D)
        m["colmask"] = cm.reshape(128, 9 * NTA)
        m["invcnt"] = inv.reshape(128, 9 * 512)
        in_maps.append(m)
    return in_maps


def run_A(inp):
    from concourse.bass_utils import run_bass_kernel_spmd
    nc = build_A()
    in_maps = host_A_inputs(inp["x"], inp["ctx"], inp["c"], inp["c_ctx"], inp["ada_w"][0], inp["ada_b"][0], inp["ev_w_in"][0], inp["ev_shift_mu"][0],
                            inp["ev_w0"][0], inp["ev_w2"][0], inp["ev_a0"][0], inp["ev_a2"][0], inp["ev_g2"][0], inp["ev_k_k"][0], inp["ev_k_a"][0],
                            inp["ev_r_k"][0], inp["ev_pool_w"][0], inp["ev_pool_scale"][0])
    res = run_bass_kernel_spmd(nc, in_maps, core_ids=list(range(8)))
    return res.results


GN_EPS = 64e-5


def build_B():
    nc = bass.Bass("TRN2", target_bir_lowering=False)
    dt = lambda n, s, k="ExternalInput", d=F32: nc.dram_tensor(n, list(s), d, kind=k).ap()
    spd = dt("sp", [2, 6, 36, 128, 256]); goutd = dt("gout", [128, 6, 2304]); bond = dt("bonus", [128, 6, 2304]); boutd = dt("bout", [128, 2, 2304])
    compd = dt("comp", [7, 2, 6, 128, 128]); xin = dt("xin", [2304, D])
    cc = dt("cc", [128, 8, 2]); adaw = dt("adaw", [D, 6 * D]); adab = dt("adab", [128, 48]); lnc = dt("lnc", [128, 4, 8])
    wout = dt("wout", [D, D]); w1 = dt("w1", [D, DFF]); w3 = dt("w3", [D, DFF]); w2 = dt("w2", [DFF, D])
    gcd = dt("gcols", [128, 12]); bod = dt("bones", [128, 128]); identd = dt("ident", [128, 128])
    x1d = dt("x1", [2048, D], "ExternalOutput"); xc1d = dt("xc1", [256, D], "ExternalOutput")
    p = Prog(nc)
    o = Ops(p)
    cx = Cx(p, nc, identd, 512)
    pb = cx.pb
    PB = lambda i: "pb%d" % i
    dm = p.sb("mod_dm", [128, 32, 2]); ln = p.sb("ln", [128, 4, 8]); gc = p.sb("gc", [128, 12]); bones = p.sb("bones", [128, 128])
    p.dma(ln[:], lnc, writes=["lnc"]); p.dma(gc[:], gcd, writes=["gc"]); p.dma(bones[:], bod, writes=["bones"])
    with p.scope():
        cx.alloc_work(8 * 128, 16)
        cx.mod_cols(cc, adaw, adab, 2, 4, "mod", dm)
    o.ts("dve", dm[:, 16:24, :], dm[:, 16:24, :], 1.0, None, ALU.add, None, ["mod"], ["mod"])
    for v in (0, 3):
        o.ts("dve", dm[:, v * 8:(v + 1) * 8, :], dm[:, v * 8:(v + 1) * 8, :], 1.0 / ALPHA, None, ALU.mult, None, ["mod"], ["mod"])
    col = lambda v, w: (lambda c: dm[:, v * 8 + c, w:w + 1])
    lcol = lambda i: (lambda c: ln[:, i, c:c + 1])
    Ybuf = p.sb("Ybuf", [128, 6, 2304])
    p.op("pool", lambda e: e.memset(Ybuf[:], 0.0), [], ["Ybuf"])

    with p.scope():
        ST = p.sb("ST", [128, 12, 64])
        blk = [p.sb("blk%d" % i, [128, 4, 64]) for i in range(4)]
        cmpb = [p.sb("cmp%d" % i, [128, 128]) for i in range(2)]
        ytmp = [p.sb("ytmp%d" % i, [128, 64]) for i in range(2)]
        p.op("pool", lambda e: e.memset(ST[:], 0.0), [], [("ST", i) for i in range(12)])
        cnt = [0]

        def sweep(d, c, chunks, tokof):
            st = ST[:, d * 6 + c, :]
            sk = ("ST", d * 6 + c)
            for ch in chunks:
                i = cnt[0] % 4
                cnt[0] += 1
                b = blk[i]
                bk = ("blk", i)
                p.dma(b[:].rearrange("p k n -> p (k n)"), spd[d, c, ch], writes=[bk])
                yb = pb[i % 2]
                sb_ = pb[2 + i % 2]
                for par in range(2):
                    pr = slice(par * 64, par * 64 + 64)
                    tp = (par * 64, par * 64)
                    o.mm(yb[pr, 0:64], st[pr, :], b[pr, 2, :], [sk, bk], [PB(i % 2)], tp=tp)
                    o.mm(sb_[pr, 0:64], b[pr, 0, :], st[pr, :], [sk, bk], [PB(2 + i % 2)], tp=tp)
                yt = ytmp[i % 2]
                t0 = tokof(ch)
                o.tt("dve", yt[:], yb[:, 0:64], b[:, 3, :], ALU.add, [PB(i % 2), bk], [("ytmp", i % 2)])
                o.tt("pool", Ybuf[:, c, t0:t0 + 64], Ybuf[:, c, t0:t0 + 64], yt[:], ALU.add, [("ytmp", i % 2), "Ybuf"], ["Ybuf"])
                o.tt("dve", st, sb_[:, 0:64], b[:, 1, :], ALU.add, [PB(2 + i % 2), bk], [sk])

        for d in range(2):
            for c in range(6):
                sweep(d, c, range(32, 36) if d == 0 else range(35, 31, -1), lambda ch: 2048 + (ch - 32) * 64)
        ci = 0
        for s in range(7):
            for d in range(2):
                for c in range(6):
                    cb = cmpb[ci % 2]
                    ck = ("cmp", ci % 2)
                    p.dma(cb[:], compd[s, d, c], writes=[ck])
                    st = ST[:, d * 6 + c, :]
                    sk = ("ST", d * 6 + c)
                    bank = pb[4 + ci % 2]
                    for par in range(2):
                        pr = slice(par * 64, par * 64 + 64)
                        o.mm(bank[pr, 0:64], cb[pr, 0:64], st[pr, :], [ck, sk], [PB(4 + ci % 2)], tp=(par * 64, par * 64))
                    o.tt("dve", st, bank[:, 0:64], cb[:, 64:128], ALU.add, [PB(4 + ci % 2), ck], [sk])
                    ci += 1
        for d in (1, 0):
            for c in range(6):
                sweep(d, c, range(32) if d == 0 else range(31, -1, -1), lambda ch: ch * 64)

    with p.scope():
        T = 512
        cx.alloc_work(NFC * 128, T)
        xT = p.sb("xT", [128, 8, T]); hT = p.sb("hT", [128, 8, T], BF16); aT = p.sb("aT", [128, 8, T], BF16); actT = p.sb("actT", [128, NFC, T], BF16)
        gl = p.sb("gl", [128, T]); bl = p.sb("bl", [128, T]); blt = p.sb("blt", [128, 2, T])
        m1 = p.sb("m1", [128, T]); m2 = p.sb("m2", [128, T]); y1 = p.sb("y1", [128, T])
        for tok0, Tn, w, dst in ((0, 512, 0, x1d), (512, 512, 0, x1d), (1024, 512, 0, x1d), (1536, 512, 0, x1d), (2048, 256, 1, xc1d)):
            cx.load_T(xin[tok0:tok0 + Tn, :], Tn, xT[:, :, 0:Tn], "xT")
            for c in range(6):
                y = Ybuf[:, c, tok0:tok0 + Tn]
                p.dma(gl[:, :Tn], goutd[:, c, tok0:tok0 + Tn], writes=["gl"])
                p.dma(bl[:, :Tn], bond[:, c, tok0:tok0 + Tn], writes=["bl"])
                o.mm(pb[0][:, :Tn], bones[:], y, ["bones", "Ybuf"], [PB(0)])
                o.tt("pool", y1[:, :Tn], y, y, ALU.mult, ["Ybuf"], ["y1"])
                o.mm(pb[1][:, :Tn], bones[:], y1[:, :Tn], ["bones", "y1"], [PB(1)])
                p.op("act", lambda e, Tn=Tn: e.mul(out=m1[:, :Tn], in_=pb[0][:, :Tn], mul=1.0 / 64), [PB(0)], ["m1"])
                o.tt("dve", m2[:, :Tn], m1[:, :Tn], m1[:, :Tn], ALU.mult, ["m1"], ["m2"])
                o.stt("dve", m2[:, :Tn], pb[1][:, :Tn], 1.0 / 64, m2[:, :Tn], ALU.mult, ALU.subtract, [PB(1), "m2"], ["m2"])
                o.ts("dve", m2[:, :Tn], m2[:, :Tn], GN_EPS, None, ALU.add, None, ["m2"], ["m2"])
                o.act(m2[:, :Tn], m2[:, :Tn], AF.Sqrt, ["m2"], ["m2"])
                p.op("dve", lambda e, Tn=Tn: e.reciprocal(out=m2[:, :Tn], in_=m2[:, :Tn]), ["m2"], ["m2"])
                o.tt("pool", y1[:, :Tn], y, m1[:, :Tn], ALU.subtract, ["Ybuf", "m1"], ["y1"])
                o.tt("dve", y1[:, :Tn], y1[:, :Tn], m2[:, :Tn], ALU.mult, ["y1", "m2"], ["y1"])
                o.act(y1[:, :Tn], y1[:, :Tn], AF.Identity, ["y1", "gc"], ["y1"], scale=gc[:, c:c + 1], bias=gc[:, 6 + c:7 + c])
                o.tt("pool", y1[:, :Tn], y1[:, :Tn], bl[:, :Tn], ALU.add, ["y1", "bl"], ["y1"])
                o.tt("dve", aT[:, c, :Tn], y1[:, :Tn], gl[:, :Tn], ALU.mult, ["y1", "gl"], [("aT", c)])
            p.dma(blt[:, :, :Tn], boutd[:, :, tok0:tok0 + Tn], writes=["blt"])
            o.cp("pool", aT[:, 6:8, :Tn], blt[:, :, :Tn], ["blt"], [("aT", 6), ("aT", 7)])
            cx.tail(Tn, cx.proj_emit(wout, 0, aT, "aT", Tn), xT[:, :, 0:Tn], "xT", col(0, w), lcol(0), lcol(1),
                    hT=hT[:, :, 0:Tn], hkey="hT", sc1=col(2, w), sh=col(1, w), modkeys=["mod"])
            cx.ffn_up(Tn, hT[:, :, 0:Tn], "hT", w1, w3, actT)
            cx.tail(Tn, cx.ffn_down_emit(Tn, w2, actT), xT[:, :, 0:Tn], "xT", col(3, w), lcol(2), lcol(3), modkeys=["mod"])
            cx.store_T(xT[:, :, 0:Tn], "xT", Tn, dst[(tok0 % 2048):(tok0 % 2048) + Tn, :])
    p.finish()
    return nc


_NC_CACHE = {}


def _get(name, fn):
    if name not in _NC_CACHE:
        _NC_CACHE[name] = fn()
    return _NC_CACHE[name]


def kernel(**inp):
    from concourse.bass_utils import run_bass_kernel_spmd
    inp = {k: np.asarray(v) for k, v in inp.items()}
    cores = list(range(8))
    mapsA = host_A_inputs(inp["x"], inp["ctx"], inp["c"], inp["c_ctx"], inp["ada_w"][0], inp["ada_b"][0], inp["ev_w_in"][0], inp["ev_shift_mu"][0],
                          inp["ev_w0"][0], inp["ev_w2"][0], inp["ev_a0"][0], inp["ev_a2"][0], inp["ev_g2"][0], inp["ev_k_k"][0], inp["ev_k_a"][0],
                          inp["ev_r_k"][0], inp["ev_pool_w"][0], inp["ev_pool_scale"][0])
    resA = run_bass_kernel_spmd(_get("A", build_A), mapsA, core_ids=cores).results
    ident_t = np.zeros((128, 128), np.float32)
    ident_t[:64, :64] = np.eye(64, dtype=np.float32)
    ident_t[64:, :64] = np.eye(64, dtype=np.float32)
    tots = [np.asarray(r["tot"]) for r in resA]
    x = inp["x"].reshape(16384, D).astype(np.float32)
    ctx = inp["ctx"].reshape(256, D).astype(np.float32)
    cc = mapsA[0]["cc"]
    lnc0 = np.ascontiguousarray(np.stack([colmajor(inp["ln_g"][0, 0]), colmajor(inp["ln_b"][0, 0]), colmajor(inp["ln_g"][0, 1]), colmajor(inp["ln_b"][0, 1])], 1))
    gcols = np.ascontiguousarray(np.concatenate([colmajor(inp["ev_lnx_g"][0]), colmajor(inp["ev_lnx_b"][0])], 1))
    sharedB = {"cc": cc, "adaw": mapsA[0]["adaw"], "adab": mapsA[0]["adab"], "lnc": lnc0, "wout": np.ascontiguousarray(inp["ev_w_out"][0]),
               "w1": np.ascontiguousarray(inp["ffn_w1"][0]), "w3": np.ascontiguousarray(inp["ffn_w3"][0]), "w2": np.ascontiguousarray(inp["ffn_w2"][0]),
               "gcols": gcols, "bones": mapsA[0]["bones"], "ident": mapsA[0]["ident"]}
    mapsB = []
    for k in range(8):
        comp = np.zeros((7, 2, 6, 128, 128), np.float32)
        comp[:, :, :] = ident_t[None, None, None]
        for s, src in enumerate(range(0, k)):
            comp[s, 0] = tots[src][0]
        for s, src in enumerate(range(7, k, -1)):
            comp[s, 1] = tots[src][1]
        m = dict(sharedB)
        m.update({"sp": resA[k]["sp"], "gout": resA[k]["gout"], "bonus": resA[k]["bonus"], "bout": resA[k]["bout"], "comp": comp,
                  "xin": np.ascontiguousarray(np.concatenate([x[k * 2048:(k + 1) * 2048], ctx], 0))})
        mapsB.append(m)
    resB = run_bass_kernel_spmd(_get("B", build_B), mapsB, core_ids=cores).results
    x1 = np.concatenate([r["x1"] for r in resB], 0)
    xc1 = np.asarray(resB[0]["xc1"])
    xp = np.concatenate([np.zeros((256, D), np.float32), x1, np.zeros((256, D), np.float32)], 0)
    lnc1 = np.ascontiguousarray(np.stack([colmajor(inp["ln_g"][1, 0]), colmajor(inp["ln_b"][1, 0]), colmajor(inp["ln_g"][1, 1]), colmajor(inp["ln_b"][1, 1])], 1))
    sharedC = {"xc1": np.ascontiguousarray(xc1), "cc": cc, "adaw": np.ascontiguousarray(inp["ada_w"][1]), "adab": colmajor(inp["ada_b"][1]),
               "lnc": lnc1, "w1": np.ascontiguousarray(inp["ffn_w1"][1]), "w3": np.ascontiguousarray(inp["ffn_w3"][1]), "w2": np.ascontiguousarray(inp["ffn_w2"][1]),
               "win": np.ascontiguousarray(inp["od_w_in"][0]), "wout": np.ascontiguousarray(inp["od_w_out"][0]), "tab": host_C_consts(inp["od_rpb"][0]),
               "ident": mapsA[0]["ident"]}
    mapsC = []
    for k in range(8):
        m = dict(sharedC)
        m["x1h"] = np.ascontiguousarray(xp[k * 2048:k * 2048 + 2560])
        m["rowmask"] = host_C_rowmask(k)
        mapsC.append(m)
    resC = run_bass_kernel_spmd(_get("C", build_C), mapsC, core_ids=cores).results
    return np.concatenate([r["out"] for r in resC], 0).reshape(1, 16384, D).astype(np.float32)
```

```python
import contextlib
import numpy as np
import concourse.bass as bass
import concourse.mybir as mybir

F32 = mybir.dt.float32
BF16 = mybir.dt.bfloat16
AF = mybir.ActivationFunctionType
ALU = mybir.AluOpType
AX = mybir.AxisListType

ENGS = ("pe", "act", "dve", "pool", "sp")
N_DMA_SEMS = 24
SAME_ENGINE_SYNC = True


class Prog:
    def __init__(self, nc):
        self.nc = nc
        self.stack = contextlib.ExitStack()
        self.root = self.stack
        self.count = {e: 0 for e in ENGS}
        self.clock = {e: {} for e in ENGS}
        self.sem = {}
        for e in ENGS:
            self.sem[e] = self.stack.enter_context(nc.semaphore("s_" + e))
        self.dsems = [self.stack.enter_context(nc.semaphore("d%d" % i)) for i in range(N_DMA_SEMS)]
        self.dcount = [0] * N_DMA_SEMS
        self.dlast = [None] * N_DMA_SEMS
        self.drr = 0
        self.res = {}
        self.n_wait = 0
        self.out_dmas = []

    def sb(self, name, shape, dt=F32):
        self.uid = getattr(self, "uid", 0) + 1
        return self.stack.enter_context(self.nc.sbuf_tensor("%s_s%d" % (name, self.uid), list(shape), dt))

    def ps(self, name, shape, dt=F32):
        self.uid = getattr(self, "uid", 0) + 1
        return self.stack.enter_context(self.nc.psum_tensor("%s_p%d" % (name, self.uid), list(shape), dt))

    def _semof(self, key):
        if isinstance(key, tuple):
            return self.dsems[key[1]]
        return self.sem[key]

    def _need(self, eng, dep, waits):
        if dep is None:
            return
        key, val, vc = dep
        if not SAME_ENGINE_SYNC and key == eng:
            return
        if key == "pe" and eng == "pe":
            return
        if self.clock[eng].get(key, 0) >= val:
            return
        cur = waits.get(key)
        if cur is None or cur[0] < val:
            waits[key] = (val, vc)

    def _deps(self, eng, reads, writes, extra=()):
        waits = {}
        for r in reads:
            st = self.res.get(r)
            if st is not None:
                self._need(eng, st["w"], waits)
        for w in writes:
            st = self.res.get(w)
            if st is not None:
                self._need(eng, st["w"], waits)
                for rd in st["r"]:
                    self._need(eng, rd, waits)
        for d in extra:
            self._need(eng, d, waits)
        items = list(waits.items())
        final = []
        for key, (val, vc) in items:
            implied = False
            for k2, (v2, vc2) in items:
                if k2 != key and vc2.get(key, 0) >= val:
                    implied = True
                    break
            if not implied:
                final.append((key, val))
        ck = self.clock[eng]
        for key, (val, vc) in items:
            for k, v in vc.items():
                if ck.get(k, 0) < v:
                    ck[k] = v
        return final

    def _record(self, me, reads, writes):
        for r in reads:
            st = self.res.setdefault(r, {"w": None, "r": []})
            st["r"].append(me)
        for w in writes:
            self.res[w] = {"w": me, "r": []}

    def _eng(self, e):
        nc = self.nc
        return {"pe": nc.tensor, "act": nc.scalar, "dve": nc.vector, "pool": nc.gpsimd, "sp": nc.sync}[e]

    def op(self, eng, fn, reads=(), writes=()):
        waits = self._deps(eng, reads, writes)
        self.count[eng] += 1
        val = self.count[eng]
        vc = dict(self.clock[eng])
        vc[eng] = val
        me = (eng, val, vc)
        self._record(me, reads, writes)
        self.n_wait += len(waits)
        eo = self._eng(eng)
        for key, v in waits:
            eo.wait_ge(self._semof(key), v)
        fn(eo).then_inc(self.sem[eng], 1)
        return me

    def dma(self, out, in_, reads=(), writes=(), q="sp", is_output=False, **kw):
        s = self.drr
        self.drr = (self.drr + 1) % N_DMA_SEMS
        extra = [self.dlast[s]] if self.dlast[s] is not None else []
        waits = self._deps(q, reads, writes, extra)
        self.dcount[s] += 16
        key = ("d", s)
        vc = dict(self.clock[q])
        vc[key] = self.dcount[s]
        me = (key, self.dcount[s], vc)
        self.dlast[s] = me
        self._record(me, reads, writes)
        self.n_wait += len(waits)
        eo = self._eng(q)
        for k, v in waits:
            eo.wait_ge(self._semof(k), v)
        eo.dma_start(out=out, in_=in_, **kw).then_inc(self.dsems[s], 16)
        if is_output:
            self.out_dmas.append(me)
        return me

    def coll(self, kind, in_ap, out_ap, reads=(), writes=()):
        q = "pool"
        if not hasattr(self, "csem"):
            self.csem = self.root.enter_context(self.nc.semaphore("cc_sem"))
            self.ccount = 0
            self.clast = None
        extra = [self.clast] if self.clast is not None else []
        waits = self._deps(q, reads, writes, extra)
        self.ccount += 1
        key = "cc"
        vc = dict(self.clock[q])
        vc[key] = self.ccount
        me = (key, self.ccount, vc)
        self.clast = me
        self.sem["cc"] = self.csem
        self._record(me, reads, writes)
        eo = self._eng(q)
        for k, v in waits:
            eo.wait_ge(self._semof(k), v)
        eo.collective_compute(kind, mybir.AluOpType.bypass, replica_groups=[list(range(8))], ins=[in_ap.opt()], outs=[out_ap.opt()]).then_inc(self.csem)
        return me

    def new_epoch(self):
        self.barrier()
        for e in ENGS:
            self.uid = getattr(self, "uid", 0) + 1
            self.sem[e] = self.root.enter_context(self.nc.semaphore("s_%s_%d" % (e, self.uid)))
            self.count[e] = 0
        self.clock = {e: {k: v for k, v in self.clock[e].items() if isinstance(k, tuple)} for e in ENGS}
        self.res = {}

    def barrier(self):
        deps = []
        for e in ENGS:
            if self.count[e] > 0:
                deps.append((e, self.count[e], {e: self.count[e]}))
        for d in self.dlast:
            if d is not None:
                deps.append((d[0], d[1], {d[0]: d[1]}))
        if getattr(self, "clast", None) is not None:
            deps.append(("cc", self.clast[1], {"cc": self.clast[1]}))
        for e in ENGS:
            waits = {}
            for d in deps:
                if d[0] == e and e != "sp":
                    if not SAME_ENGINE_SYNC or e == "pe":
                        continue
                self._need(e, d, waits)
            eo = self._eng(e)
            for key, (val, vc) in waits.items():
                eo.wait_ge(self._semof(key), val)
                self.clock[e][key] = max(self.clock[e].get(key, 0), val)
        self.res = {}

    def finish(self):
        for d in self.dlast:
            if d is not None and self.clock["sp"].get(d[0], 0) < d[1]:
                self.nc.sync.wait_ge(self._semof(d[0]), d[1])
        self.stack.close()

    @contextlib.contextmanager
    def scope(self):
        old = self.stack
        self.stack = contextlib.ExitStack()
        try:
            yield
        finally:
            self.barrier()
            self.stack.close()
            self.stack = old


D = 1024
DFF = 2816
NFC = 22
ALPHA = 4.0 ** 0.25
EPSP = 1e-6 / (ALPHA * ALPHA)
NEG = -30000.0
_eng_rr = [0]


class Cx:
    def __init__(self, p, nc, ident_ap, TT):
        self.p, self.nc, self.TT = p, nc, TT
        self.pb = [p.ps("pb%d" % i, [128, 512]) for i in range(8)]
        self.ident = p.sb("ident", [128, 128])
        self.ones = p.sb("ones", [128, 128])
        self.onesb = p.sb("onesb", [128, 64], BF16)
        p.dma(self.ident[:], ident_ap, writes=["ident"])
        p.op("pool", lambda e: e.memset(self.ones[:], 1.0), [], ["ones"])
        p.op("pool", lambda e: e.memset(self.onesb[:], 1.0), [], ["onesb"])
        self.wi = 0

    def alloc_work(self, wst_cols, TT):
        p = self.p
        self.TT = TT
        self.wst = [p.sb("wst%d" % i, [128, wst_cols]) for i in range(2)]
        self.wbf = [p.sb("wbf%d" % i, [128, wst_cols], BF16) for i in range(2)]
        self.tb = p.sb("tb", [128, 8, TT])
        self.sq = [p.sb("sq%d" % i, [128, TT]) for i in range(2)]
        self.st = [p.sb("st%d" % i, [128, TT]) for i in range(5)]
        self.io = [p.sb("io%d" % i, [128, 1024]) for i in range(2)]
        self.ioi = 0

    def wblock(self, w_ap, kc, col0, ncols=128, cast=True):
        p = self.p
        i = self.wi
        self.wi ^= 1
        st = self.wst[i][:, :kc * ncols].rearrange("p (k n) -> p k n", n=ncols)
        src = w_ap[:, col0:col0 + ncols].rearrange("(k p) n -> p k n", p=128)
        p.dma(st, src, writes=[("wst", i)])
        if not cast:
            return st, ("wst", i)
        bf = self.wbf[i][:, :kc * ncols].rearrange("p (k n) -> p k n", n=ncols)
        if i == 0:
            p.op("act", lambda e: e.copy(out=bf, in_=st), [("wst", i)], [("wbf", i)])
        else:
            p.op("pool", lambda e: e.tensor_copy(out=bf, in_=st), [("wst", i)], [("wbf", i)])
        return bf, ("wbf", i)

    def load_T(self, rows_ap, ntok, dst, dkey, t0=0):
        p = self.p
        s0 = 0
        while s0 < ntok:
            n = min(128, ntok - s0)
            io = self.io[self.ioi]
            iok = ("io", self.ioi)
            self.ioi ^= 1
            p.dma(io[0:n, :], rows_ap[s0:s0 + n, :], writes=[iok])
            for hb in range(2):
                bank = self.pb[6 + hb]
                bk = "pb%d" % (6 + hb)
                for c4 in range(4):
                    c = hb * 4 + c4
                    p.op("pe", lambda e, c=c, c4=c4, bank=bank, io=io, n=n: e.transpose(bank[:, c4 * 128:c4 * 128 + n], io[0:n, c * 128:(c + 1) * 128], self.ident[0:n, 0:n]),
                         [iok, "ident"], [bk])
                out = dst[:, hb * 4:hb * 4 + 4, t0 + s0:t0 + s0 + n]
                src = bank[:].rearrange("p (a b) -> p a b", b=128)[:, :, 0:n]
                p.op("act" if hb == 0 else "dve",
                     (lambda e, out=out, src=src: e.copy(out=out, in_=src)) if hb == 0 else (lambda e, out=out, src=src: e.tensor_copy(out=out, in_=src)),
                     [bk], [(dkey, hb * 4 + i) for i in range(4)])
            s0 += n

    def store_T(self, src, skey, ntok, rows_ap, t0=0, is_output=True):
        p = self.p
        for s in range(ntok // 128):
            io = self.io[self.ioi]
            iok = ("io", self.ioi)
            self.ioi ^= 1
            for hb in range(2):
                bank = self.pb[6 + hb]
                bk = "pb%d" % (6 + hb)
                for c4 in range(4):
                    c = hb * 4 + c4
                    p.op("pe", lambda e, c=c, c4=c4, bank=bank: e.transpose(bank[:, c4 * 128:(c4 + 1) * 128], src[:, c, t0 + s * 128:t0 + (s + 1) * 128], self.ident[:]),
                         [(skey, c), "ident"], [bk])
                if hb == 0:
                    p.op("act", lambda e, io=io, bank=bank: e.copy(out=io[:, 0:512], in_=bank[:]), [bk], [iok])
                else:
                    p.op("dve", lambda e, io=io, bank=bank: e.tensor_copy(out=io[:, 512:1024], in_=bank[:]), [bk], [iok])
            p.dma(rows_ap[s * 128:(s + 1) * 128, :], io[:], reads=[iok], is_output=is_output)

    def mod_cols(self, cc_ap, adaw_ap, adab_ap, v0, nv, name, dm):
        p = self.p
        cc = p.sb(name + "_cc", [128, 8, 2])
        sg = p.sb(name + "_sg", [128, 8, 2])
        ab = p.sb(name + "_ab", [128, 48])
        p.dma(cc[:], cc_ap, writes=[name + "cc"])
        p.dma(ab[:], adab_ap, writes=[name + "ab"])
        p.op("act", lambda e: e.activation(out=sg[:], in_=cc[:], func=AF.Sigmoid), [name + "cc"], [name + "sg"])
        p.op("dve", lambda e: e.tensor_tensor(out=cc[:], in0=cc[:], in1=sg[:], op=ALU.mult), [name + "cc", name + "sg"], [name + "cc"])
        bank = self.pb[5]
        for m in range(nv * 8):
            wb, wk = self.wblock(adaw_ap, 8, (v0 * 8 + m) * 128, cast=False)
            for k in range(8):
                p.op("pe", lambda e, m=m, k=k, wb=wb: e.matmul(bank[:, 2 * m:2 * m + 2], lhsT=wb[:, k, :], rhs=cc[:, k, :], start=(k == 0), stop=(k == 7)),
                     [wk, name + "cc"], ["pb5"])
        bsrc = ab[:, v0 * 8:(v0 + nv) * 8]
        p.op("dve", lambda e: e.tensor_tensor(out=dm[:], in0=bank[:, 0:nv * 16].rearrange("p (m t) -> p m t", t=2),
                                              in1=bsrc.unsqueeze(2).to_broadcast([128, nv * 8, 2]), op=ALU.add),
             ["pb5", name + "ab"], [name])
        return dm

    def tail(self, T, emit_y, xT, xkey, gcol, lng, lnb, hT=None, hkey=None, sc1=None, sh=None, modkeys=()):
        p = self.p
        tb = self.tb
        for c in range(8):
            yb = self.pb[4 + (c % 2)]
            yk = "pb%d" % (4 + c % 2)
            emit_y(c, yb[:, :T], yk)
            p.op("dve", lambda e, c=c, yb=yb: e.scalar_tensor_tensor(out=tb[:, c, :T], in0=yb[:, :T], scalar=gcol(c), in1=xT[:, c, :], op0=ALU.mult, op1=ALU.add),
                 [yk, (xkey, c)] + list(modkeys), [("tb", c)])
            sq = self.sq[c % 2]
            p.op("pool", lambda e, c=c, sq=sq: e.tensor_tensor(out=sq[:, :T], in0=tb[:, c, :T], in1=tb[:, c, :T], op=ALU.mult), [("tb", c)], [("sq", c % 2)])
            p.op("pe", lambda e, c=c: e.matmul(self.pb[6][:, :T], lhsT=self.ones[:], rhs=tb[:, c, :T], start=(c == 0), stop=(c == 7)), [("tb", c), "ones"], ["pb6"])
            p.op("pe", lambda e, c=c, sq=sq: e.matmul(self.pb[7][:, :T], lhsT=self.ones[:], rhs=sq[:, :T], start=(c == 0), stop=(c == 7)), [("sq", c % 2), "ones"], ["pb7"])
        mean, msq, var, std, rstd = [t[:, :T] for t in self.st]
        p.op("act", lambda e: e.mul(out=mean, in_=self.pb[6][:, :T], mul=1.0 / D), ["pb6"], ["st0"])
        p.op("dve", lambda e: e.tensor_tensor(out=msq, in0=mean, in1=mean, op=ALU.mult), ["st0"], ["st1"])
        p.op("dve", lambda e: e.scalar_tensor_tensor(out=var, in0=self.pb[7][:, :T], scalar=1.0 / D, in1=msq, op0=ALU.mult, op1=ALU.subtract), ["pb7", "st1"], ["st2"])
        p.op("dve", lambda e: e.tensor_scalar(out=var, in0=var, scalar1=EPSP, scalar2=None, op0=ALU.add), ["st2"], ["st2"])
        p.op("act", lambda e: e.activation(out=std, in_=var, func=AF.Sqrt), ["st2"], ["st3"])
        p.op("dve", lambda e: e.reciprocal(out=rstd, in_=std), ["st3"], ["st4"])
        for c in range(8):
            p.op("pool", lambda e, c=c: e.tensor_tensor(out=tb[:, c, :T], in0=tb[:, c, :T], in1=mean, op=ALU.subtract), [("tb", c), "st0"], [("tb", c)])
            p.op("dve", lambda e, c=c: e.tensor_tensor(out=tb[:, c, :T], in0=tb[:, c, :T], in1=rstd, op=ALU.mult), [("tb", c), "st4"], [("tb", c)])
            p.op("act", lambda e, c=c: e.activation(out=xT[:, c, :], in_=tb[:, c, :T], func=AF.Identity, scale=lng(c), bias=lnb(c)),
                 [("tb", c), "lnc"], [(xkey, c)])
            if hT is not None:
                p.op("dve", lambda e, c=c: e.tensor_scalar(out=hT[:, c, :], in0=xT[:, c, :], scalar1=sc1(c), scalar2=sh(c), op0=ALU.mult, op1=ALU.add),
                     [(xkey, c)] + list(modkeys), [(hkey, c)])

    def modulate(self, T, xT, xkey, hT, hkey, sc1, sh, modkeys=()):
        for c in range(8):
            self.p.op("dve" if c % 2 else "pool", lambda e, c=c: e.tensor_scalar(out=hT[:, c, :], in0=xT[:, c, :], scalar1=sc1(c), scalar2=sh(c), op0=ALU.mult, op1=ALU.add),
                      [(xkey, c)] + list(modkeys), [(hkey, c)])

    def ffn_up(self, T, hT, hkey, w1, w3, actT):
        p = self.p
        for f in range(NFC):
            b1, k1 = self.wblock(w1, 8, f * 128)
            b3, k3 = self.wblock(w3, 8, f * 128)
            u1 = self.pb[0 + 2 * (f % 2)]
            u3 = self.pb[1 + 2 * (f % 2)]
            n1 = "pb%d" % (0 + 2 * (f % 2))
            n3 = "pb%d" % (1 + 2 * (f % 2))
            for k in range(8):
                p.op("pe", lambda e, k=k, b1=b1, u1=u1: e.matmul(u1[:, :T], lhsT=b1[:, k, :], rhs=hT[:, k, :], start=(k == 0), stop=(k == 7)), [k1, (hkey, k)], [n1])
            for k in range(8):
                p.op("pe", lambda e, k=k, b3=b3, u3=u3: e.matmul(u3[:, :T], lhsT=b3[:, k, :], rhs=hT[:, k, :], start=(k == 0), stop=(k == 7)), [k3, (hkey, k)], [n3])
            sl = self.sq[f % 2]
            p.op("act", lambda e, u1=u1, sl=sl: e.activation(out=sl[:, :T], in_=u1[:, :T], func=AF.Silu), [n1], [("sq", f % 2)])
            p.op("dve", lambda e, u3=u3, sl=sl, f=f: e.tensor_tensor(out=actT[:, f, :T], in0=u3[:, :T], in1=sl[:, :T], op=ALU.mult), [n3, ("sq", f % 2)], [("actT", f)])

    def ffn_down_emit(self, T, w2, actT):
        def emit(c, ps, pk):
            b2, k2 = self.wblock(w2, NFC, c * 128)
            for f in range(NFC):
                self.p.op("pe", lambda e, f=f, b2=b2: e.matmul(ps, lhsT=b2[:, f, :], rhs=actT[:, f, :T], start=(f == 0), stop=(f == NFC - 1)), [k2, ("actT", f)], [pk])
        return emit

    def proj_emit(self, w_ap, col_base, rhsT, rkey, T, kc=8):
        def emit(c, ps, pk):
            b, bk = self.wblock(w_ap, kc, col_base + c * 128)
            for k in range(kc):
                self.p.op("pe", lambda e, k=k, b=b: e.matmul(ps, lhsT=b[:, k, :], rhs=rhsT[:, k, :T], start=(k == 0), stop=(k == kc - 1)), [bk, (rkey, k)], [pk])
        return emit


def emit_C(nc, p, cx, T):
    xc1, cc, adaw, adab, lnc = T["xc1loc"], T["cc"], T["adaw1"], T["adab1"], T["lnc1"]
    w1, w3, w2, win, wout = T["w1_1"], T["w3_1"], T["w2_1"], T["win1"], T["wout1"]
    tabd, rmd, out, xmid = T["tab"], T["rowmask"], T["out"], T["xmid"]
    x1loc, halo = T["x1loc"], T["halo"]

    def rows(u0, n):
        if u0 < 256:
            return halo[u0:u0 + n, :]
        if u0 < 2304:
            return x1loc[u0 - 256:u0 - 256 + n, :]
        return halo[256 + u0 - 2304:256 + u0 - 2304 + n, :]
    dm = cx_mod = None
    ln = p.sb("ln", [128, 4, 8])
    p.dma(ln[:], lnc, writes=["lnc"])
    dm = p.sb("mod_dm", [128, 48, 2])
    with p.scope():
        cx.alloc_work(8 * 128, 256)
        cx.mod_cols(cc, adaw, adab, 0, 6, "mod", dm)
    for v in (1, 4):
        p.op("dve", lambda e, v=v: e.tensor_scalar(out=dm[:, v * 8:(v + 1) * 8, :], in0=dm[:, v * 8:(v + 1) * 8, :], scalar1=1.0, scalar2=None, op0=ALU.add), ["mod"], ["mod"])
    for v in (2, 5):
        p.op("dve", lambda e, v=v: e.tensor_scalar(out=dm[:, v * 8:(v + 1) * 8, :], in0=dm[:, v * 8:(v + 1) * 8, :], scalar1=1.0 / ALPHA, scalar2=None, op0=ALU.mult), ["mod"], ["mod"])
    col = lambda v, w: (lambda c: dm[:, v * 8 + c, w:w + 1])
    lcol = lambda i: (lambda c: ln[:, i, c:c + 1])

    with p.scope():
        cx.alloc_work(8 * 128, 256)
        T = 256
        kT = p.sb("kT", [128, 8, 2560], BF16)
        vtm = p.sb("vtm", [128, 20, D], BF16)
        kcT = p.sb("kcT", [128, 8, 256], BF16)
        vctm = p.sb("vctm", [128, 2, D], BF16)
        tab = p.sb("tab", [128, 15, 16, 64], BF16)
        rm = p.sb("rm", [128, 8, 6, 4])
        for dd in range(15):
            io = cx.io[dd % 2]
            p.dma(io[:], tabd[:, dd * 1024:(dd + 1) * 1024], writes=[("io", dd % 2)])
            p.op("pool", lambda e, dd=dd, io=io: e.tensor_copy(out=tab[:, dd, :, :].rearrange("p a b -> p (a b)"), in_=io[:]), [("io", dd % 2)], ["tab"])
        p.dma(rm[:].rearrange("p a b c -> p (a b c)"), rmd, writes=["rm"])
        xt = p.sb("xt", [128, 8, T])
        hT = p.sb("hT", [128, 8, T], BF16)
        idb = p.sb("idb", [128, 128], BF16)
        p.op("dve", lambda e: e.tensor_copy(out=idb[:], in_=cx.ident[:]), ["ident"], ["idb"])
        pbv = p.ps_alias = None

        with p.scope():
            TK = 512
            xtk = p.sb("xtk", [128, 8, TK]); hTk = p.sb("hTk", [128, 8, TK], BF16); vTk = p.sb("vTk", [128, 8, TK], BF16)

            def kv_pass(rows_fn, ntok, w_idx, kdst, vdst, t_base, ch_base):
                for t0 in range(0, ntok, TK):
                    Tn = min(TK, ntok - t0)
                    for h0 in range(0, Tn, 256):
                        cx.load_T(rows_fn(t0 + h0, 256), 256, xtk[:, :, h0:h0 + 256], "xt")
                    cx.modulate(Tn, xtk[:, :, 0:Tn], "xt", hTk[:, :, 0:Tn], "hT", col(1, w_idx), col(0, w_idx), ["mod"])
                    for m in range(8):
                        b, bk = cx.wblock(win, 8, D + m * 128)
                        ps = cx.pb[m % 2]; pk = "pb%d" % (m % 2)
                        for k in range(8):
                            p.op("pe", lambda e, k=k, b=b, ps=ps, Tn=Tn: e.matmul(ps[:, :Tn], lhsT=b[:, k, :], rhs=hTk[:, k, 0:Tn], start=(k == 0), stop=(k == 7)), [bk, ("hT", k)], [pk])
                        p.op("act", lambda e, m=m, ps=ps, Tn=Tn, t0=t0: e.copy(out=kdst[:, m, t_base + t0:t_base + t0 + Tn], in_=ps[:, :Tn]), [pk], [("kT", m)])
                    for m in range(8):
                        b, bk = cx.wblock(win, 8, 2 * D + m * 128)
                        ps = cx.pb[m % 2]; pk = "pb%d" % (m % 2)
                        for k in range(8):
                            p.op("pe", lambda e, k=k, b=b, ps=ps, Tn=Tn: e.matmul(ps[:, :Tn], lhsT=b[:, k, :], rhs=hTk[:, k, 0:Tn], start=(k == 0), stop=(k == 7)), [bk, ("hT", k)], [pk])
                        p.op("act", lambda e, m=m, ps=ps, Tn=Tn: e.copy(out=vTk[:, m, 0:Tn], in_=ps[:, :Tn]), [pk], [("vT", m)])
                    for s in range(Tn // 128):
                        psb = cx.pb[2 + s % 2][:].bitcast(BF16)
                        pk = "pb%d" % (2 + s % 2)
                        for m in range(8):
                            p.op("pe", lambda e, m=m, s=s, psb=psb: e.transpose(psb[:, m * 128:(m + 1) * 128], vTk[:, m, s * 128:(s + 1) * 128], idb[:]), [("vT", m), "idb"], [pk])
                        ch = ch_base + (t0 // 128) + s
                        p.op("dve", lambda e, ch=ch, psb=psb: e.tensor_copy(out=vdst[:, ch, :], in_=psb), [pk], [("vtm", ch)])

            kv_pass(rows, 2560, 0, kT, vtm, 0, 0)
            kv_pass(lambda u0, n: xc1[u0:u0 + n, :], 256, 1, kcT, vctm, 0, 0)
        p.barrier()

        qT = p.sb("qT", [128, 8, T], BF16)
        oT = p.sb("oT", [128, 8, T], BF16)
        pT = [p.sb("pT%d" % i, [128, T], BF16) for i in range(8)]
        stmp = [p.sb("stmp%d" % i, [128, T]) for i in range(8)]
        rden = p.sb("rden", [128, T])
        slot = [0]
        for b in range(8):
            q0 = 256 + b * 256
            cx.load_T(rows(q0, T), T, xt[:], "xt")
            cx.modulate(T, xt[:], "xt", hT[:], "hT", col(1, 0), col(0, 0), ["mod"])
            for m in range(8):
                bw, bk = cx.wblock(win, 8, m * 128)
                ps = cx.pb[6 + m % 2]; pk = "pb%d" % (6 + m % 2)
                for k in range(8):
                    p.op("pe", lambda e, k=k, bw=bw, ps=ps: e.matmul(ps[:, :T], lhsT=bw[:, k, :], rhs=hT[:, k, :], start=(k == 0), stop=(k == 7)), [bk, ("hT", k)], [pk])
                p.op("act", lambda e, m=m, ps=ps: e.mul(out=qT[:, m, :], in_=ps[:, :T], mul=0.125), [pk], [("qT", m)])
            for c in range(8):
                for par in range(2):
                    h = 2 * c + par
                    pr = slice(par * 64, par * 64 + 64)
                    sl_of = []
                    for j in range(8):
                        sl = slot[0] % 8
                        slot[0] += 1
                        sps = cx.pb[sl // 2][:, (sl % 2) * 256:(sl % 2) * 256 + 256]
                        sk = ("S", sl)
                        if j < 6:
                            u0 = b * 256 + j * 128
                            lhs = kT[pr, c, u0:u0 + 128]
                        else:
                            lhs = kcT[pr, c, (j - 6) * 128:(j - 5) * 128]
                        p.op("pe", lambda e, lhs=lhs, sps=sps, pr=pr, c=c: e.matmul(sps, lhsT=lhs, rhs=qT[pr, c, :], start=True, stop=True, tile_position=(pr.start, 0)), [("kT", c), ("qT", c)], [sk])
                        sl_of.append((sl, sps, sk))
                    for j in range(8):
                        sl, sps, sk = sl_of[j]
                        pt = pT[sl]
                        if j < 6:
                            tmp = stmp[sl]
                            tk = ("stmp", sl)
                            d0 = 11 - 2 * j
                            p.op("dve", lambda e, tmp=tmp, sps=sps, d0=d0, h=h: e.tensor_tensor(out=tmp[:].rearrange("p (a b) -> p a b", b=64), in0=sps.rearrange("p (a b) -> p a b", b=64),
                                                                                   in1=tab[:, d0:d0 + 4, h, :], op=ALU.add), [sk, "tab"], [tk])
                            p.op("pool", lambda e, tmp=tmp, b=b, j=j: e.tensor_tensor(out=tmp[:].rearrange("p (a b) -> p a b", b=64), in0=tmp[:].rearrange("p (a b) -> p a b", b=64),
                                                                                    in1=rm[:, b, j, :].unsqueeze(2).to_broadcast([128, 4, 64]), op=ALU.add), [tk, "rm"], [tk])
                            p.op("act", lambda e, tmp=tmp, pt=pt: e.activation(out=pt[:], in_=tmp[:], func=AF.Exp), [tk], [("pT", sl)])
                        else:
                            p.op("act", lambda e, sps=sps, pt=pt: e.activation(out=pt[:], in_=sps, func=AF.Exp), [sk], [("pT", sl)])
                    for j in range(8):
                        sl, sps, sk = sl_of[j]
                        pt = pT[sl]
                        if j < 6:
                            vl = vtm[:, 2 * b + j, h * 64:(h + 1) * 64]
                            vk = ("vtm", 2 * b + j)
                        else:
                            vl = vctm[:, j - 6, h * 64:(h + 1) * 64]
                            vk = ("vtm", j - 6)
                        nps = cx.pb[4][pr, :T]
                        dps = cx.pb[5][pr, :T]
                        p.op("pe", lambda e, vl=vl, pt=pt, nps=nps, j=j, par=par: e.matmul(nps, lhsT=vl, rhs=pt[:], start=(j == 0), stop=(j == 7), tile_position=(0, par * 64)), [vk, ("pT", sl)], [("num", par)])
                        p.op("pe", lambda e, pt=pt, dps=dps, j=j, par=par: e.matmul(dps, lhsT=cx.onesb[:], rhs=pt[:], start=(j == 0), stop=(j == 7), tile_position=(0, par * 64)), ["onesb", ("pT", sl)], [("den", par)])
                p.op("dve", lambda e: e.reciprocal(out=rden[:], in_=cx.pb[5][:, :T]), [("den", 0), ("den", 1)], ["rden"])
                p.op("dve", lambda e, c=c: e.tensor_tensor(out=oT[:, c, :], in0=cx.pb[4][:, :T], in1=rden[:], op=ALU.mult), [("num", 0), ("num", 1), "rden"], [("oT", c)])
            def emit_wo(c, ps, pk):
                bw, bk = cx.wblock(wout, 8, c * 128)
                for k in range(8):
                    p.op("pe", lambda e, k=k, bw=bw: e.matmul(ps, lhsT=bw[:, k, :], rhs=oT[:, k, :], start=(k == 0), stop=(k == 7)),
                         [bk, ("oT", k)], [pk, ("num", 0), ("num", 1), ("den", 0), ("den", 1)])
            cx.tail(T, emit_wo, xt[:], "xt", col(2, 0), lcol(0), lcol(1), modkeys=["mod"])
            p.dma(xmid[:, :, b * 256:(b + 1) * 256], xt[:], reads=[("xt", c) for c in range(8)], writes=["xmid"])
            p.barrier()

    with p.scope():
        T = 512
        cx.alloc_work(NFC * 128, T)
        xt = p.sb("xt3", [128, 8, T])
        hT = p.sb("hT3", [128, 8, T], BF16)
        actT = p.sb("actT", [128, NFC, T], BF16)
        for t in range(4):
            p.dma(xt[:], xmid[:, :, t * T:(t + 1) * T], reads=["xmid"], writes=[("xt", c) for c in range(8)])
            cx.modulate(T, xt[:], "xt", hT[:], "hT", col(4, 0), col(3, 0), ["mod"])
            cx.ffn_up(T, hT[:], "hT", w1, w3, actT)
            cx.tail(T, cx.ffn_down_emit(T, w2, actT), xt[:], "xt", col(5, 0), lcol(2), lcol(3), modkeys=["mod"])
            cx.store_T(xt[:], "xt", T, out[t * T:(t + 1) * T, :])


def colmajor(v):
    v = np.asarray(v, np.float32)
    return np.ascontiguousarray(v.reshape(-1, 128).T)


def host_C_consts(rpb):
    rpb = np.asarray(rpb, np.float32)
    tab = np.full((2, 64, 15, 16, 64), NEG, np.float32)
    cq = np.arange(64)
    sc = np.clip(cq - 8, 0, 48)
    cp = np.arange(64)[:, None]
    valid = (cp >= sc[None, :]) & (cp < sc[None, :] + 16)
    coff = np.clip(cp - cq[None, :] + 15, 0, 30)
    for ipar in range(2):
        for dprime in range(15):
            delta = 7 - dprime + ipar
            ro = delta + 7
            if ro < 0 or ro > 14:
                continue
            for h in range(16):
                g = rpb[h, ro][coff]
                tab[ipar, :, dprime, h, :] = np.where(valid, g, np.float32(NEG))
    return tab.reshape(128, 15 * 16 * 64)


def host_C_rowmask(core):
    rm = np.full((2, 64, 8, 6, 4), NEG, np.float32)
    R0 = core * 32
    for b in range(8):
        for j in range(6):
            for ipar in range(2):
                krow = R0 + 4 * b - 4 + 2 * j + ipar
                for rq in range(4):
                    qrow = R0 + 4 * b + rq
                    sr = min(max(qrow - 4, 0), 248)
                    if 0 <= krow < 256 and sr <= krow < sr + 8:
                        rm[ipar, :, b, j, rq] = 0.0
    return rm.reshape(128, 8 * 6 * 4)


class Ops:
    def __init__(self, p):
        self.p = p

    def tt(self, eng, out, in0, in1, op, R, W):
        return self.p.op(eng, lambda e: e.tensor_tensor(out=out, in0=in0, in1=in1, op=op), R, W)

    def ts(self, eng, out, in0, s1, s2, op0, op1, R, W):
        if s2 is None:
            return self.p.op(eng, lambda e: e.tensor_scalar(out=out, in0=in0, scalar1=s1, scalar2=None, op0=op0), R, W)
        return self.p.op(eng, lambda e: e.tensor_scalar(out=out, in0=in0, scalar1=s1, scalar2=s2, op0=op0, op1=op1), R, W)

    def stt(self, eng, out, in0, scalar, in1, op0, op1, R, W):
        return self.p.op(eng, lambda e: e.scalar_tensor_tensor(out=out, in0=in0, scalar=scalar, in1=in1, op0=op0, op1=op1), R, W)

    def act(self, out, in_, func, R, W, scale=None, bias=None):
        kw = {}
        if scale is not None:
            kw["scale"] = scale
        if bias is not None:
            kw["bias"] = bias
        return self.p.op("act", lambda e: e.activation(out=out, in_=in_, func=func, **kw), R, W)

    def cp(self, eng, out, in_, R, W):
        if eng == "act":
            return self.p.op("act", lambda e: e.copy(out=out, in_=in_), R, W)
        return self.p.op(eng, lambda e: e.tensor_copy(out=out, in_=in_), R, W)

    def mm(self, out, lhsT, rhs, R, W, start=True, stop=True, tp=None):
        if tp is None:
            return self.p.op("pe", lambda e: e.matmul(out, lhsT=lhsT, rhs=rhs, start=start, stop=stop), R, W)
        return self.p.op("pe", lambda e: e.matmul(out, lhsT=lhsT, rhs=rhs, start=start, stop=stop, tile_position=tp), R, W)

    def tr(self, out, in_, ident, R, W):
        return self.p.op("pe", lambda e: e.transpose(out, in_, ident), R, W)


NTA = 272
C_LWS = -0.6065306597126334


def emit_A(nc, p, cx, T):
    xh, cmd, invd, cc, adaw, adab, win, pcd = T["xh"], T["colmask"], T["invcnt"], T["cc"], T["adaw0"], T["adab0"], T["win0"], T["pcols"]
    w2d, a2d, g2d, pwd, bod, mkd, identd = T["w2t"], T["a2t"], T["g2"], T["poolw"], T["bones"], T["masks"], T["ident"]
    spd, goutd, bond, boutd = T["sp"], T["gout"], T["bonus"], T["bout"]
    totd = T["totb"].rearrange("(d c p) n -> d c p n", d=2, c=6)
    o = Ops(p)
    pb = cx.pb
    PB = lambda i: "pb%d" % i
    dm = p.sb("mod_dm", [128, 16, 2])
    pc = p.sb("pc", [128, 86]); w2t = p.sb("w2t", [128, 768]); a2t = p.sb("a2t", [128, 768]); g2t = p.sb("g2t", [128, 768])
    pw = p.sb("pw", [128, 2, 128]); bones = p.sb("bones", [128, 128]); mk = p.sb("mk", [64, 4, 512])
    cm = p.sb("cm", [128, 9, NTA]); AB = p.sb("AB", [128, 12, 128]); id2 = p.sb("id2", [128, 64])
    for t_, d_, k_ in ((pc, pcd, "pc"), (w2t, w2d, "w2t"), (a2t, a2d, "a2t"), (g2t, g2d, "g2t"), (bones, bod, "bones")):
        p.dma(t_[:], d_, writes=[k_])
    p.dma(pw[:].rearrange("p a b -> p (a b)"), pwd, writes=["pw"])
    p.dma(mk[:].rearrange("p a b -> p (a b)"), mkd, writes=["mk"])
    p.dma(cm[:].rearrange("p a b -> p (a b)"), cmd, writes=["cm"])
    p.dma(id2[0:64, :], identd[0:64, 0:64], writes=["id2"])
    p.dma(id2[64:128, :], identd[0:64, 0:64], writes=["id2"])
    with p.scope():
        cx.alloc_work(8 * 128, 16)
        cx.mod_cols(cc, adaw, adab, 0, 2, "mod", dm)
    o.ts("dve", dm[:, 8:16, :], dm[:, 8:16, :], 1.0, None, ALU.add, None, ["mod"], ["mod"])
    col = lambda v, w: (lambda c: dm[:, v * 8 + c, w:w + 1])
    MU0, MU1, W0, A0, KK_, KA_, RK_, PS_, A0C = 0, 21, 42, 54, 66, 72, 78, 84, None
    a0c = p.sb("a0c", [128, 21])
    o.tt("dve", a0c[:], pc[:, 0:21], pc[:, 21:42], ALU.add, ["pc"], ["a0c"])
    o.ts("dve", a0c[:], a0c[:], -1.0, 1.0, ALU.mult, ALU.add, ["a0c"], ["a0c"])
    pcol = lambda base, i: pc[:, base + i:base + i + 1]
    for i in range(12):
        o.cp("pool", AB[:, i, 0:64], id2[:], ["id2"], [("AB", i)])
        p.op("pool", lambda e, i=i: e.memset(AB[:, i, 64:128], 0.0), [], [("AB", i)])
    MSK = {"Us": 0, "Ui": 1, "Ls": 2, "Li": 3}

    with p.scope():
        cx.alloc_work(8 * 128, 16)
        xT = p.sb("xT", [128, 8, NTA]); hT = p.sb("hT", [128, 8, NTA], BF16)
        P = p.sb("P", [128, 23, NTA]); Q = p.sb("Q", [128, 21, 256])
        inv = p.sb("inv", [128, 2, 256]); pa = p.sb("pa", [128, NTA]); pbf = p.sb("pbf", [128, NTA]); pooled = p.sb("pooled", [128, 2, 256])
        bst = p.sb("bst", [128, 2, 256])
        tw = p.sb("tw", [128, 256]); sg = p.sb("sg", [128, 256]); gst = p.sb("gst", [128, 256]); bon = p.sb("bon", [128, 256])
        kk = p.sb("kk", [128, 256]); t1 = p.sb("t1", [128, 256]); t2 = p.sb("t2", [128, 256])
        lw = [p.sb("lw%d" % d, [128, 256]) for d in range(2)]; aa = [p.sb("aa%d" % d, [128, 256]) for d in range(2)]
        kd = [p.sb("kd%d" % d, [128, 256]) for d in range(2)]; bb = [p.sb("bb%d" % d, [128, 256]) for d in range(2)]
        cl = p.sb("cl", [128, 256]); cle = p.sb("cle", [128, 256]); E = p.sb("E", [128, 256]); totc = p.sb("totc", [128, 4]); clc = p.sb("clc", [128, 4]); PC = p.sb("PC", [128, 4])
        kkt = p.sb("kkt", [128, 256]); rt = p.sb("rt", [128, 256]); bt = p.sb("bt", [128, 256]); kt = p.sb("kt", [128, 256]); bh = p.sb("bh", [128, 256]); kh = p.sb("kh", [128, 256])
        Vtm = p.sb("Vtm", [64, 4, 128]); BHtm = p.sb("BHtm", [64, 4, 128]); KHtm = p.sb("KHtm", [64, 4, 128]); X = p.sb("X", [64, 4, 2, 128])
        Lb = [p.sb("Lb%d" % i, [64, 512]) for i in range(2)]; LTb = [p.sb("LTb%d" % i, [64, 512]) for i in range(2)]
        AkkT = p.sb("AkkT", [64, 512]); AbT = p.sb("AbT", [64, 512]); AkT = p.sb("AkT", [64, 512])
        ost = p.sb("ost", [128, 4, 4, 64]); Mb = p.sb("Mb", [128, 4, 64]); idpc = p.sb("idpc", [128, 4, 64])
        v4 = lambda t: t[:].rearrange("p (q n) -> p q n", n=64)

        def stage_dc(d, c, ti, r, v):
            fwd = (d == 0)
            for q in range(4):
                p.op("dve", lambda e, q=q: e.tensor_tensor_scan(out=cl[:, q * 64:(q + 1) * 64], data0=cx.ones[:, 0:64], data1=lw[d][:, q * 64:(q + 1) * 64], initial=0.0, op0=ALU.mult, op1=ALU.add),
                     [("lw", d), "ones"], ["cl"])
            if not fwd:
                o.cp("pool", totc[:], v4(cl)[:, :, 63], ["cl"], ["totc"])
                o.tt("pool", cle[:], lw[d][:], cl[:], ALU.subtract, [("lw", d), "cl"], ["cle"])
                o.tt("dve", v4(cl), v4(cle), totc[:].unsqueeze(2).to_broadcast([128, 4, 64]), ALU.add, ["cle", "totc"], ["cl"])
            o.tt("pool", cle[:], cl[:], lw[d][:], ALU.subtract, ["cl", ("lw", d)], ["cle"])
            o.cp("pool", clc[:], v4(cl)[:, :, 63 if fwd else 0], ["cl"], ["clc"])
            o.act(E[:], cle[:], AF.Exp, ["cle"], ["E"])
            o.tt("dve", kkt[:], kk[:], E[:], ALU.mult, ["kk", "E"], ["kkt"])
            o.act(E[:], cl[:], AF.Exp, ["cl"], ["E"])
            o.tt("dve", rt[:], r, E[:], ALU.mult, ["Q", "E"], ["rt"])
            o.act(E[:], cl[:], AF.Exp, ["cl"], ["E"], scale=-1.0)
            o.tt("dve", bt[:], bb[d][:], E[:], ALU.mult, [("bb", d), "E"], ["bt"])
            o.tt("pool", kt[:], kd[d][:], E[:], ALU.mult, [("kd", d), "E"], ["kt"])
            o.tt("dve", v4(cle), clc[:].unsqueeze(2).to_broadcast([128, 4, 64]), v4(cl), ALU.subtract, ["clc", "cl"], ["cle"])
            o.act(E[:], cle[:], AF.Exp, ["cle"], ["E"])
            o.tt("dve", bh[:], bb[d][:], E[:], ALU.mult, [("bb", d), "E"], ["bh"])
            o.tt("pool", kh[:], kd[d][:], E[:], ALU.mult, [("kd", d), "E"], ["kh"])
            o.act(PC[:], clc[:], AF.Exp, ["clc"], ["PC"])
            for q in range(4):
                o.tr(pb[0][0:64, q * 128:(q + 1) * 128], kkt[:, q * 64:(q + 1) * 64], cx.ident[:], ["kkt", "ident"], [PB(0)])
                o.tr(pb[1][0:64, q * 128:(q + 1) * 128], bh[:, q * 64:(q + 1) * 64], cx.ident[:], ["bh", "ident"], [PB(1)])
                o.tr(pb[2][0:64, q * 128:(q + 1) * 128], kh[:, q * 64:(q + 1) * 64], cx.ident[:], ["kh", "ident"], [PB(2)])
            o.cp("act", X[:, :, :, 0:64], pb[0][0:64, :].rearrange("p (q a n) -> p q a n", a=2, n=64), [PB(0)], ["X"])
            o.cp("dve", BHtm[:], pb[1][0:64, :].rearrange("p (q n) -> p q n", n=128), [PB(1)], ["BHtm"])
            o.cp("pool" if False else "act", KHtm[:], pb[2][0:64, :].rearrange("p (q n) -> p q n", n=128), [PB(2)], ["KHtm"])
            for q in range(4):
                for par in range(2):
                    pr = slice(par * 64, par * 64 + 64)
                    qs = slice(q * 64, q * 64 + 64)
                    cs = slice((q * 2 + par) * 64, (q * 2 + par) * 64 + 64)
                    tp = (par * 64, 0)
                    o.mm(pb[0][0:64, cs], bt[pr, qs], kkt[pr, qs], ["bt", "kkt"], [PB(0)], tp=tp)
                    o.mm(pb[1][0:64, cs], kkt[pr, qs], bt[pr, qs], ["bt", "kkt"], [PB(1)], tp=tp)
                    o.mm(pb[2][0:64, cs], kt[pr, qs], kkt[pr, qs], ["kt", "kkt"], [PB(2)], tp=tp)
                    o.mm(pb[4][0:64, cs], bt[pr, qs], rt[pr, qs], ["bt", "rt"], [PB(4)], tp=tp)
                    o.mm(pb[5][0:64, cs], kt[pr, qs], rt[pr, qs], ["kt", "rt"], [PB(5)], tp=tp)
            mS_st = mk[:, MSK["Us" if fwd else "Ls"], :]
            mS_ts = mk[:, MSK["Ls" if fwd else "Us"], :]
            mI_st = mk[:, MSK["Ui" if fwd else "Li"], :]
            o.tt("dve", LTb[0][:], pb[0][0:64, :], mS_st, ALU.mult, [PB(0), "mk"], [("LT", 0)])
            o.tt("pool", Lb[0][:], pb[1][0:64, :], mS_ts, ALU.mult, [PB(1), "mk"], [("L", 0)]) if False else o.tt("dve", Lb[0][:], pb[1][0:64, :], mS_ts, ALU.mult, [PB(1), "mk"], [("L", 0)])
            o.tt("dve", AkkT[:], pb[2][0:64, :], mS_st, ALU.mult, [PB(2), "mk"], ["AkkT"])
            o.tt("dve", AbT[:], pb[4][0:64, :], mI_st, ALU.mult, [PB(4), "mk"], ["AbT"])
            o.tt("dve", AkT[:], pb[5][0:64, :], mI_st, ALU.mult, [PB(5), "mk"], ["AkT"])
            for q in range(4):
                for par in range(2):
                    cs = slice((q * 2 + par) * 64, (q * 2 + par) * 64 + 64)
                    o.mm(pb[6][0:64, cs], AkkT[:, cs], Vtm[:, q, par * 64:(par + 1) * 64], ["AkkT", "Vtm"], [PB(6)])
            o.ts("dve", X[:, :, :, 64:128], pb[6][0:64, :].rearrange("p (q a n) -> p q a n", a=2, n=64), -1.0, None, ALU.mult, None, [PB(6)], ["X"])

            def apply(LTcur, ltkey, sub):
                for idx in range(8):
                    q, par = idx // 2, idx % 2
                    bank = 6 + idx // 4
                    o.mm(pb[bank][0:64, (idx % 4) * 128:(idx % 4 + 1) * 128], LTcur[:, idx * 64:(idx + 1) * 64], X[:, q, par, :], [ltkey, "X"], [PB(bank)])
                for hf in range(2):
                    o.tt("dve", X[:, 2 * hf:2 * hf + 2, :, :], X[:, 2 * hf:2 * hf + 2, :, :], pb[6 + hf][0:64, :].rearrange("p (q a n) -> p q a n", a=2, n=128),
                         ALU.subtract if sub else ALU.add, ["X", PB(6 + hf)], ["X"])

            apply(LTb[0], ("LT", 0), True)
            cur = 0
            for k in range(5):
                nxt = cur ^ 1
                for idx in range(8):
                    cs = slice(idx * 64, idx * 64 + 64)
                    o.mm(pb[0][0:64, cs], LTb[cur][:, cs], Lb[cur][:, cs], [("LT", cur), ("L", cur)], [PB(0)])
                    o.mm(pb[1][0:64, cs], Lb[cur][:, cs], LTb[cur][:, cs], [("LT", cur), ("L", cur)], [PB(1)])
                o.cp("act", Lb[nxt][:], pb[0][0:64, :], [PB(0)], [("L", nxt)])
                o.cp("dve", LTb[nxt][:], pb[1][0:64, :], [PB(1)], [("LT", nxt)])
                apply(LTb[nxt], ("LT", nxt), False)
                cur = nxt
            for q in range(4):
                for par in range(2):
                    pr = slice(par * 64, par * 64 + 64)
                    qs = slice(q * 64, q * 64 + 64)
                    cs = slice((q * 2 + par) * 64, (q * 2 + par) * 64 + 64)
                    tp = (0, par * 64)
                    Wt = X[:, q, par, 0:64]
                    Ut = X[:, q, par, 64:128]
                    BHq = BHtm[:, q, par * 64:(par + 1) * 64]
                    KHq = KHtm[:, q, par * 64:(par + 1) * 64]
                    Vq = Vtm[:, q, par * 64:(par + 1) * 64]
                    o.mm(pb[2][pr, qs], Wt, BHq, ["X", "BHtm"], [PB(2)], tp=tp)
                    o.mm(pb[2][pr, 256 + q * 64:256 + q * 64 + 64], BHq, Ut, ["X", "BHtm"], [PB(2)], start=True, stop=False, tp=tp)
                    o.mm(pb[2][pr, 256 + q * 64:256 + q * 64 + 64], KHq, Vq, ["KHtm", "Vtm"], [PB(2)], start=False, stop=True, tp=tp)
                    o.mm(pb[4][pr, qs], Wt, AbT[:, cs], ["X", "AbT"], [PB(4)], tp=tp)
                    o.mm(pb[4][pr, 256 + q * 64:256 + q * 64 + 64], Ut, AbT[:, cs], ["X", "AbT"], [PB(4)], start=True, stop=False, tp=tp)
                    o.mm(pb[4][pr, 256 + q * 64:256 + q * 64 + 64], Vq, AkT[:, cs], ["Vtm", "AkT"], [PB(4)], start=False, stop=True, tp=tp)
                    if not fwd:
                        o.mm(pb[5][pr, qs], BHq, Wt, ["X", "BHtm"], [PB(5)], tp=tp)
            o.tt("pool", idpc[:], id2[:].unsqueeze(1).to_broadcast([128, 4, 64]), PC[:].unsqueeze(2).to_broadcast([128, 4, 64]), ALU.mult, ["id2", "PC"], ["idpc"])
            o.tt("dve", ost[:, :, 0, :], idpc[:], pb[2][:, 0:256].rearrange("p (q n) -> p q n", n=64), ALU.subtract, ["idpc", PB(2)], ["ost"])
            o.cp("act", ost[:, :, 1, :], pb[2][:, 256:512].rearrange("p (q n) -> p q n", n=64), [PB(2)], ["ost"])
            o.tt("dve", ost[:, :, 2, :], v4(rt), pb[4][:, 0:256].rearrange("p (q n) -> p q n", n=64), ALU.subtract, ["rt", PB(4)], ["ost"])
            o.cp("act", ost[:, :, 3, :], pb[4][:, 256:512].rearrange("p (q n) -> p q n", n=64), [PB(4)], ["ost"])
            ch0 = ti * 4
            p.dma(spd[d, c, ch0:ch0 + 4].rearrange("q p x -> p q x"), ost[:].rearrange("p q k n -> p q (k n)"), reads=["ost"])
            if ti >= 8:
                return
            if not fwd:
                o.tt("dve", Mb[:], idpc[:], pb[5][:, 0:256].rearrange("p (q n) -> p q n", n=64), ALU.subtract, ["idpc", PB(5)], ["Mb"])
            ab = AB[:, d * 6 + c, :]
            abk = ("AB", d * 6 + c)
            for q in range(4):
                for par in range(2):
                    pr = slice(par * 64, par * 64 + 64)
                    tp = (par * 64, par * 64)
                    if fwd:
                        o.mm(pb[3][pr, 0:128], ost[pr, q, 0, :], ab[pr, :], ["ost", abk], [PB(3)], tp=tp)
                    else:
                        o.mm(pb[3][pr, 0:64], Mb[pr, q, :], ab[pr, 0:64], ["Mb", abk], [PB(3)], tp=tp)
                        o.mm(pb[3][pr, 64:128], ab[pr, 0:64], ost[pr, q, 1, :], ["ost", abk], [PB(3)], tp=tp)
                o.cp("act", ab[:, 0:64], pb[3][:, 0:64], [PB(3)], [abk])
                if fwd:
                    o.tt("dve", ab[:, 64:128], pb[3][:, 64:128], ost[:, q, 1, :], ALU.add, [PB(3), "ost"], [abk])
                else:
                    o.tt("dve", ab[:, 64:128], pb[3][:, 64:128], ab[:, 64:128], ALU.add, [PB(3), abk], [abk])

        for ti in range(9):
            w = 0 if ti < 8 else 1
            tok0 = ti * 256
            cx.load_T(xh[ti * NTA:(ti + 1) * NTA, :], NTA, xT[:], "xT")
            cx.modulate(NTA, xT[:], "xT", hT[:], "hT", col(1, w), col(0, w), ["mod"])
            p.dma(inv[:].rearrange("p a b -> p (a b)"), invd[:, ti * 512:(ti + 1) * 512], writes=["inv"])
            for m in range(23):
                bw, bk = cx.wblock(win, 8, m * 128)
                ps = pb[m % 2]
                for k in range(8):
                    o.mm(ps[:, :NTA], bw[:, k, :], hT[:, k, :], [bk, ("hT", k)], [PB(m % 2)], start=(k == 0), stop=(k == 7))
                o.tt("dve", P[:, m, :], ps[:, :NTA], cm[:, ti, :], ALU.mult, [PB(m % 2), "cm"], [("P", m)])
            for m in range(21):
                o.act(Q[:, m, :], P[:, m, 8:264], AF.Identity, [("P", m), "a0c"], ["Q"], scale=a0c[:, m:m + 1])
                o.stt("dve", Q[:, m, :], P[:, m, 7:263], pcol(MU0, m), Q[:, m, :], ALU.mult, ALU.add, [("P", m), "pc", "Q"], ["Q"])
                o.stt("dve", Q[:, m, :], P[:, m, 9:265], pcol(MU1, m), Q[:, m, :], ALU.mult, ALU.add, [("P", m), "pc", "Q"], ["Q"])
            for gi, (m, par, wdw) in enumerate(((21, 0, 2), (21, 1, 4), (22, 0, 8), (22, 1, 16))):
                pr = slice(par * 64, par * 64 + 64)
                mi = m - 21
                eng = "pool" if gi % 2 else "dve"
                src = P[pr, m, :]
                cur, curk = src, ("P", m)
                bufs = [(pa[pr, :], ("pa", par)), (pbf[pr, :], ("pbf", par))]
                n, k, i = NTA, 1, 0
                while k < wdw:
                    n2 = n - k
                    o.tt(eng, bufs[i][0][:, 0:n2], cur[:, 0:n2], cur[:, k:k + n2], ALU.add, [curk], [bufs[i][1]])
                    cur, curk = bufs[i]
                    i ^= 1
                    n = n2
                    k *= 2
                x0 = 8 - wdw // 2
                o.tt(eng, pooled[pr, mi, :], cur[:, x0:x0 + 256], inv[pr, mi, :], ALU.mult, [curk, "inv"], [("pooled", gi)])
                o.tt(eng, pooled[pr, mi, :], pooled[pr, mi, :], src[:, 8:264], ALU.subtract, [("pooled", gi), ("P", m)], [("pooled", gi)])
            for mi in range(2):
                o.mm(pb[2 + mi][:, 0:256], pw[:, mi, :], pooled[:, mi, :], ["pw", ("pooled", 2 * mi), ("pooled", 2 * mi + 1)], [PB(2 + mi)])
                o.ts("dve", bst[:, mi, :], pb[2 + mi][:, 0:256], pcol(PS_, mi), None, ALU.mult, None, [PB(2 + mi), "pc"], ["bst"])
            p.dma(boutd[:, :, tok0:tok0 + 256], bst[:], reads=["bst"])
            o.act(tw[:], Q[:, 18, :], AF.Tanh, ["Q"], ["tw"])
            o.act(sg[:], Q[:, 20, :], AF.Sigmoid, ["Q"], ["sg"])
            for c in range(6):
                r = Q[:, c, :]
                k_ = Q[:, 6 + c, :]
                v = Q[:, 12 + c, :]
                o.mm(pb[2][:, 0:256], g2t[:, c * 128:(c + 1) * 128], sg[:], ["g2t", "sg"], [PB(2)])
                o.cp("act", gst[:], pb[2][:, 0:256], [PB(2)], ["gst"])
                p.dma(goutd[:, c, tok0:tok0 + 256], gst[:], reads=["gst"])
                o.ts("dve", t1[:], k_, pcol(KK_, c), None, ALU.mult, None, ["Q", "pc"], ["t1"])
                o.tt("pool", t2[:], t1[:], t1[:], ALU.mult, ["t1"], ["t2"])
                o.mm(pb[3][:, 0:256], bones[:], t2[:], ["bones", "t2"], [PB(3)])
                o.act(t2[:], pb[3][:, 0:256], AF.Sqrt, [PB(3)], ["t2"])
                o.ts("dve", t2[:], t2[:], 1e-12, None, ALU.max, None, ["t2"], ["t2"])
                p.op("dve", lambda e: e.reciprocal(out=t2[:], in_=t2[:]), ["t2"], ["t2"])
                o.tt("dve", kk[:], t1[:], t2[:], ALU.mult, ["t1", "t2"], ["kk"])
                for d in range(2):
                    dr = slice(d * 64, d * 64 + 64)
                    o.mm(pb[2][:, 0:256], w2t[dr, c * 128:(c + 1) * 128], tw[dr, :], ["w2t", "tw"], [PB(2)], tp=(d * 64, 0))
                    o.act(lw[d][:], pb[2][:, 0:256], AF.Sigmoid, [PB(2), "pc"], [("lw", d)], bias=pcol(W0, d * 6 + c))
                    o.ts("pool", lw[d][:], lw[d][:], C_LWS, None, ALU.mult, None, [("lw", d)], [("lw", d)])
                    o.mm(pb[3][:, 0:256], a2t[dr, c * 128:(c + 1) * 128], Q[dr, 19, :], ["a2t", "Q"], [PB(3)], tp=(d * 64, 0))
                    o.act(aa[d][:], pb[3][:, 0:256], AF.Sigmoid, [PB(3), "pc"], [("aa", d)], bias=pcol(A0, d * 6 + c))
                    o.ts("dve", t1[:], aa[d][:], -1.0, pcol(KA_, c), ALU.add, ALU.mult, [("aa", d), "pc"], ["t1"])
                    o.stt("dve", kd[d][:], t1[:], 1.0, k_, ALU.add, ALU.mult, ["t1", "Q"], [("kd", d)])
                    o.tt("pool", bb[d][:], kk[:], aa[d][:], ALU.mult, ["kk", ("aa", d)], [("bb", d)])
                o.tt("dve", t1[:], kd[0][:], kd[1][:], ALU.add, [("kd", 0), ("kd", 1)], ["t1"])
                o.stt("dve", t1[:], t1[:], pcol(RK_, c), r, ALU.mult, ALU.mult, ["t1", "pc", "Q"], ["t1"])
                o.mm(pb[2][:, 0:256], bones[:], t1[:], ["bones", "t1"], [PB(2)])
                o.tt("dve", bon[:], pb[2][:, 0:256], v, ALU.mult, [PB(2), "Q"], ["bon"])
                p.dma(bond[:, c, tok0:tok0 + 256], bon[:], reads=["bon"])
                for q in range(4):
                    o.tr(pb[3][0:64, q * 128:(q + 1) * 128], v[:, q * 64:(q + 1) * 64], cx.ident[:], ["Q", "ident"], [PB(3)])
                o.cp("act", Vtm[:], pb[3][0:64, :].rearrange("p (q n) -> p q n", n=128), [PB(3)], ["Vtm"])
                for d in range(2):
                    stage_dc(d, c, ti, r, v)
        for c in range(6):
            ab = AB[:, c, :]
            for par in range(2):
                pr = slice(par * 64, par * 64 + 64)
                o.mm(pb[3][pr, 0:64], ab[pr, 0:64], cx.ident[pr, pr], [("AB", c), "ident"], [PB(3)], tp=(par * 64, par * 64))
            o.cp("act", ab[:, 0:64], pb[3][:, 0:64], [PB(3)], [("AB", c)])
        for i in range(12):
            p.dma(totd[i // 6, i % 6], AB[:, i, :], reads=[("AB", i)])


def host_A_inputs(x, ctx, c, c_ctx, ada_w0, ada_b0, win, shift_mu, w0, w2, a0, a2, g2, k_k, k_a, r_k, pool_w, pool_scale):
    x = np.asarray(x, np.float32).reshape(16384, D)
    ctx = np.asarray(ctx, np.float32).reshape(256, D)
    xp = np.concatenate([np.zeros((8, D), np.float32), x, np.zeros((8, D), np.float32)], 0)
    cp_ = np.concatenate([np.zeros((8, D), np.float32), ctx, np.zeros((8, D), np.float32)], 0)
    cc = np.ascontiguousarray(np.stack([colmajor(c.reshape(-1)), colmajor(c_ctx.reshape(-1))], -1))
    pcols = np.concatenate([colmajor(shift_mu[0]), colmajor(shift_mu[1]), colmajor(w0.reshape(-1)), colmajor(a0.reshape(-1)),
                            colmajor(k_k), colmajor(k_a), colmajor(r_k.reshape(-1)), colmajor(pool_scale)], 1)
    assert pcols.shape == (128, 86)
    pw = np.zeros((128, 2, 128), np.float32)
    for g in range(4):
        mi, par = g // 2, g % 2
        pw[par * 64:(par + 1) * 64, mi, par * 64:(par + 1) * 64] = pool_w[g]
    bones = np.zeros((128, 128), np.float32)
    bones[:64, :64] = 1.0
    bones[64:, 64:] = 1.0
    pp = np.arange(64)[:, None]
    ff = np.arange(512)[None, :] % 64
    masks = np.stack([(pp < ff), (pp <= ff), (pp > ff), (pp >= ff)], 1).astype(np.float32)
    shared = {"cc": cc, "adaw": np.ascontiguousarray(ada_w0), "adab": colmajor(ada_b0), "win": np.ascontiguousarray(win), "pcols": np.ascontiguousarray(pcols),
              "w2t": np.ascontiguousarray(w2.reshape(128, 768)), "a2t": np.ascontiguousarray(a2.reshape(128, 768)), "g2": np.ascontiguousarray(g2),
              "poolw": pw.reshape(128, 256), "bones": bones, "masks": np.ascontiguousarray(masks.reshape(64, 2048)), "ident": np.eye(128, dtype=np.float32)}
    wins = (2, 4, 8, 16)
    in_maps = []
    for k in range(8):
        xh = np.zeros((9, NTA, D), np.float32)
        cm = np.zeros((128, 9, NTA), np.float32)
        inv = np.ones((128, 9, 2, 256), np.float32)
        for ti in range(9):
            if ti < 8:
                t0 = k * 2048 + ti * 256
                xh[ti] = xp[t0:t0 + NTA]
                L = 16384
            else:
                t0 = 0
                xh[ti] = cp_
                L = 256
            tok = t0 - 8 + np.arange(NTA)
            cm[:, ti, :] = ((tok >= 0) & (tok < L)).astype(np.float32)[None, :]
            t = t0 + np.arange(256)
            for g in range(4):
                mi, par = g // 2, g % 2
                lo = np.clip(t - wins[g] // 2, 0, L)
                hi = np.clip(t + wins[g] // 2, 0, L)
                inv[par * 64:(par + 1) * 64, ti, mi, :] = (np.float32(1.0) / (hi - lo).astype(np.float32))[None, :]
        m = dict(shared)
        m["xh"] = xh.reshape(9 * NTA, D)
        m["colmask"] = cm.reshape(128, 9 * NTA)
        m["invcnt"] = inv.reshape(128, 9 * 512)
        in_maps.append(m)
    return in_maps


GN_EPS = 64e-5


def emit_B(nc, p, cx, T):
    spd, goutd, bond, boutd, xin = T["sp"], T["gout"], T["bonus"], T["bout"], T["xin"]
    totg = T["totg"].rearrange("(r d c p) n -> r d c p n", r=8, d=2, c=6)
    cc, adaw, adab, lnc = T["cc"], T["adaw0"], T["adab0"], T["lnc0"]
    wout, w1, w3, w2 = T["wout0"], T["w1_0"], T["w3_0"], T["w2_0"]
    gcd, bod, identd, cmkd = T["gcols"], T["bones"], T["ident"], T["cmask"]
    x1d, xc1d = T["x1loc"], T["xc1loc"]
    o = Ops(p)
    pb = cx.pb
    PB = lambda i: "pb%d" % i
    dm = p.sb("mod_dm", [128, 32, 2]); ln = p.sb("ln", [128, 4, 8]); gc = p.sb("gc", [128, 12]); bones = p.sb("bones", [128, 128])
    p.dma(ln[:], lnc, writes=["lnc"]); p.dma(gc[:], gcd, writes=["gc"]); p.dma(bones[:], bod, writes=["bones"])
    with p.scope():
        cx.alloc_work(8 * 128, 16)
        cx.mod_cols(cc, adaw, adab, 2, 4, "mod", dm)
    o.ts("dve", dm[:, 16:24, :], dm[:, 16:24, :], 1.0, None, ALU.add, None, ["mod"], ["mod"])
    for v in (0, 3):
        o.ts("dve", dm[:, v * 8:(v + 1) * 8, :], dm[:, v * 8:(v + 1) * 8, :], 1.0 / ALPHA, None, ALU.mult, None, ["mod"], ["mod"])
    col = lambda v, w: (lambda c: dm[:, v * 8 + c, w:w + 1])
    lcol = lambda i: (lambda c: ln[:, i, c:c + 1])
    Ybuf = p.sb("Ybuf", [128, 6, 2304])
    p.op("pool", lambda e: e.memset(Ybuf[:], 0.0), [], [("Ybuf", c) for c in range(6)])

    with p.scope():
        ST = p.sb("ST", [128, 12, 64])
        blk = [p.sb("blk%d" % i, [128, 4, 64]) for i in range(4)]
        cmpb = [p.sb("cmp%d" % i, [128, 128]) for i in range(2)]
        ytmp = [p.sb("ytmp%d" % i, [128, 64]) for i in range(2)]
        p.op("pool", lambda e: e.memset(ST[:], 0.0), [], [("ST", i) for i in range(12)])
        cnt = [0]

        def sweep1(d, c, ch, tokof):
            st = ST[:, d * 6 + c, :]
            sk = ("ST", d * 6 + c)
            if True:
                i = cnt[0] % 4
                cnt[0] += 1
                b = blk[i]
                bk = ("blk", i)
                p.dma(b[:].rearrange("p k n -> p (k n)"), spd[d, c, ch], writes=[bk])
                yb = pb[i % 2]
                sb_ = pb[2 + i % 2]
                for par in range(2):
                    pr = slice(par * 64, par * 64 + 64)
                    tp = (par * 64, par * 64)
                    o.mm(yb[pr, 0:64], st[pr, :], b[pr, 2, :], [sk, bk], [PB(i % 2)], tp=tp)
                    o.mm(sb_[pr, 0:64], b[pr, 0, :], st[pr, :], [sk, bk], [PB(2 + i % 2)], tp=tp)
                yt = ytmp[i % 2]
                t0 = tokof(ch)
                o.tt("dve", yt[:], yb[:, 0:64], b[:, 3, :], ALU.add, [PB(i % 2), bk], [("ytmp", i % 2)])
                o.tt("pool", Ybuf[:, c, t0:t0 + 64], Ybuf[:, c, t0:t0 + 64], yt[:], ALU.add, [("ytmp", i % 2), ("Ybuf", c)], [("Ybuf", c)])
                o.tt("dve", st, sb_[:, 0:64], b[:, 1, :], ALU.add, [PB(2 + i % 2), bk], [sk])

        for i_ in range(4):
            for d in range(2):
                for c in range(6):
                    sweep1(d, c, 32 + i_ if d == 0 else 35 - i_, lambda ch: 2048 + (ch - 32) * 64)
        cmk = p.sb("cmk", [128, 16])
        ctmp = p.sb("ctmp", [128, 64])
        p.dma(cmk[:], cmkd, writes=["cmk"])
        ci = 0
        for d in range(2):
            for s in (range(8) if d == 0 else range(7, -1, -1)):
                for c in range(6):
                    cb = cmpb[ci % 2]
                    ck = ("cmp", ci % 2)
                    p.dma(cb[:], totg[s, d, c], writes=[ck])
                    st = ST[:, d * 6 + c, :]
                    sk = ("ST", d * 6 + c)
                    bank = pb[4 + ci % 2]
                    for par in range(2):
                        pr = slice(par * 64, par * 64 + 64)
                        o.mm(bank[pr, 0:64], cb[pr, 0:64], st[pr, :], [ck, sk], [PB(4 + ci % 2)], tp=(par * 64, par * 64))
                    o.tt("dve", ctmp[:], bank[:, 0:64], cb[:, 64:128], ALU.add, [PB(4 + ci % 2), ck], ["ctmp"])
                    o.tt("dve", ctmp[:], ctmp[:], st, ALU.subtract, ["ctmp", sk], ["ctmp"])
                    o.stt("dve", st, ctmp[:], cmk[:, d * 8 + s:d * 8 + s + 1], st, ALU.mult, ALU.add, ["ctmp", "cmk", sk], [sk])
                    ci += 1
        for i_ in range(32):
            for d in range(2):
                for c in range(6):
                    sweep1(d, c, i_ if d == 0 else 31 - i_, lambda ch: ch * 64)

    with p.scope():
        T = 512
        cx.alloc_work(NFC * 128, T)
        xT = p.sb("xT", [128, 8, T]); hT = p.sb("hT", [128, 8, T], BF16); aT = p.sb("aT", [128, 8, T], BF16); actT = p.sb("actT", [128, NFC, T], BF16)
        gl = p.sb("gl", [128, T]); bl = p.sb("bl", [128, T]); blt = p.sb("blt", [128, 2, T])
        m1 = p.sb("m1", [128, T]); m2 = p.sb("m2", [128, T]); y1 = p.sb("y1", [128, T])
        for tok0, Tn, w, dst in ((0, 512, 0, x1d), (512, 512, 0, x1d), (1024, 512, 0, x1d), (1536, 512, 0, x1d), (2048, 256, 1, xc1d)):
            cx.load_T(xin[tok0:tok0 + Tn, :], Tn, xT[:, :, 0:Tn], "xT")
            for c in range(6):
                y = Ybuf[:, c, tok0:tok0 + Tn]
                p.dma(gl[:, :Tn], goutd[:, c, tok0:tok0 + Tn], writes=["gl"])
                p.dma(bl[:, :Tn], bond[:, c, tok0:tok0 + Tn], writes=["bl"])
                o.mm(pb[0][:, :Tn], bones[:], y, ["bones", ("Ybuf", c)], [PB(0)])
                o.tt("pool", y1[:, :Tn], y, y, ALU.mult, [("Ybuf", c)], ["y1"])
                o.mm(pb[1][:, :Tn], bones[:], y1[:, :Tn], ["bones", "y1"], [PB(1)])
                p.op("act", lambda e, Tn=Tn: e.mul(out=m1[:, :Tn], in_=pb[0][:, :Tn], mul=1.0 / 64), [PB(0)], ["m1"])
                o.tt("dve", m2[:, :Tn], m1[:, :Tn], m1[:, :Tn], ALU.mult, ["m1"], ["m2"])
                o.stt("dve", m2[:, :Tn], pb[1][:, :Tn], 1.0 / 64, m2[:, :Tn], ALU.mult, ALU.subtract, [PB(1), "m2"], ["m2"])
                o.ts("dve", m2[:, :Tn], m2[:, :Tn], GN_EPS, None, ALU.add, None, ["m2"], ["m2"])
                o.act(m2[:, :Tn], m2[:, :Tn], AF.Sqrt, ["m2"], ["m2"])
                p.op("dve", lambda e, Tn=Tn: e.reciprocal(out=m2[:, :Tn], in_=m2[:, :Tn]), ["m2"], ["m2"])
                o.tt("pool", y1[:, :Tn], y, m1[:, :Tn], ALU.subtract, [("Ybuf", c), "m1"], ["y1"])
                o.tt("dve", y1[:, :Tn], y1[:, :Tn], m2[:, :Tn], ALU.mult, ["y1", "m2"], ["y1"])
                o.act(y1[:, :Tn], y1[:, :Tn], AF.Identity, ["y1", "gc"], ["y1"], scale=gc[:, c:c + 1], bias=gc[:, 6 + c:7 + c])
                o.tt("pool", y1[:, :Tn], y1[:, :Tn], bl[:, :Tn], ALU.add, ["y1", "bl"], ["y1"])
                o.tt("dve", aT[:, c, :Tn], y1[:, :Tn], gl[:, :Tn], ALU.mult, ["y1", "gl"], [("aT", c)])
            p.dma(blt[:, :, :Tn], boutd[:, :, tok0:tok0 + Tn], writes=["blt"])
            o.cp("pool", aT[:, 6:8, :Tn], blt[:, :, :Tn], ["blt"], [("aT", 6), ("aT", 7)])
            cx.tail(Tn, cx.proj_emit(wout, 0, aT, "aT", Tn), xT[:, :, 0:Tn], "xT", col(0, w), lcol(0), lcol(1),
                    hT=hT[:, :, 0:Tn], hkey="hT", sc1=col(2, w), sh=col(1, w), modkeys=["mod"])
            cx.ffn_up(Tn, hT[:, :, 0:Tn], "hT", w1, w3, actT)
            cx.tail(Tn, cx.ffn_down_emit(Tn, w2, actT), xT[:, :, 0:Tn], "xT", col(3, w), lcol(2), lcol(3), modkeys=["mod"])
            cx.store_T(xT[:, :, 0:Tn], "xT", Tn, dst[(tok0 % 2048):(tok0 % 2048) + Tn, :], is_output=False)


IN_SPECS = [
    ("xh", [9 * NTA, D]), ("colmask", [128, 9 * NTA]), ("invcnt", [128, 9 * 512]), ("cc", [128, 8, 2]),
    ("adaw0", [D, 6 * D]), ("adab0", [128, 48]), ("win0", [D, 2944]), ("pcols", [128, 86]), ("w2t", [128, 768]), ("a2t", [128, 768]),
    ("g2", [128, 768]), ("poolw", [128, 256]), ("bones", [128, 128]), ("masks", [64, 2048]), ("ident", [128, 128]),
    ("xin", [2304, D]), ("lnc0", [128, 4, 8]), ("wout0", [D, D]), ("w1_0", [D, DFF]), ("w3_0", [D, DFF]), ("w2_0", [DFF, D]),
    ("gcols", [128, 12]), ("cmask", [128, 16]), ("hmask", [128, 16]),
    ("adaw1", [D, 6 * D]), ("adab1", [128, 48]), ("lnc1", [128, 4, 8]), ("w1_1", [D, DFF]), ("w3_1", [D, DFF]), ("w2_1", [DFF, D]),
    ("win1", [D, 3 * D]), ("wout1", [D, D]), ("tab", [128, 15 * 16 * 64]), ("rowmask", [128, 8 * 6 * 4]),
]
SCRATCH = [("sp", [2, 6, 36, 128, 256]), ("gout", [128, 6, 2304]), ("bonus", [128, 6, 2304]), ("bout", [128, 2, 2304]),
           ("totb", [1536, 128]), ("totg", [8 * 1536, 128]), ("x1loc", [2048, D]), ("xc1loc", [256, D]),
           ("bnd", [512, D]), ("bndg", [8 * 512, D]), ("halo", [512, D]), ("xmid", [128, 8, 2048])]


def build_fused():
    nc = bass.Bass("TRN2", target_bir_lowering=False)
    T = {}
    for n, sh in IN_SPECS:
        T[n] = nc.dram_tensor(n, list(sh), F32, kind="ExternalInput").ap()
    for n, sh in SCRATCH:
        T[n] = nc.dram_tensor(n, list(sh), F32).ap()
    T["out"] = nc.dram_tensor("out", [2048, D], F32, kind="ExternalOutput").ap()
    p = Prog(nc)
    cx = Cx(p, nc, T["ident"], 512)
    with p.scope():
        emit_A(nc, p, cx, T)
    p.new_epoch()
    p.coll("AllGather", T["totb"], T["totg"])
    p.new_epoch()
    with p.scope():
        emit_B(nc, p, cx, T)
    p.new_epoch()
    p.dma(T["bnd"][0:256, :], T["x1loc"][0:256, :])
    p.dma(T["bnd"][256:512, :], T["x1loc"][1792:2048, :])
    p.barrier()
    p.coll("AllGather", T["bnd"], T["bndg"])
    p.new_epoch()
    with p.scope():
        hm = p.sb("hm", [128, 16])
        acc = p.sb("hacc", [128, D])
        src = [p.sb("hsrc%d" % i, [128, D]) for i in range(2)]
        p.dma(hm[:], T["hmask"], writes=["hm"])
        bg = T["bndg"].rearrange("(r h) n -> r h n", r=8)
        n = 0
        for half in range(2):
            for sub in range(2):
                p.op("pool", lambda e: e.memset(acc[:], 0.0), [], ["hacc"])
                for s in range(8):
                    st_ = src[n % 2]
                    sk = ("hsrc", n % 2)
                    n += 1
                    r0 = (256 if half == 0 else 0) + sub * 128
                    p.dma(st_[:], bg[s, r0:r0 + 128, :], writes=[sk])
                    p.op("dve", lambda e, st_=st_, s=s, half=half: e.scalar_tensor_tensor(out=acc[:], in0=st_[:], scalar=hm[:, half * 8 + s:half * 8 + s + 1], in1=acc[:], op0=ALU.mult, op1=ALU.add),
                         [sk, "hm", "hacc"], ["hacc"])
                p.dma(T["halo"][half * 256 + sub * 128:half * 256 + sub * 128 + 128, :], acc[:], reads=["hacc"])
    p.new_epoch()
    with p.scope():
        emit_C(nc, p, cx, T)
    p.finish()
    return nc


_NC_CACHE = {}


def kernel(**inp):
    from concourse.bass_utils import run_bass_kernel_spmd
    inp = {k: np.asarray(v) for k, v in inp.items()}
    mapsA = host_A_inputs(inp["x"], inp["ctx"], inp["c"], inp["c_ctx"], inp["ada_w"][0], inp["ada_b"][0], inp["ev_w_in"][0], inp["ev_shift_mu"][0],
                          inp["ev_w0"][0], inp["ev_w2"][0], inp["ev_a0"][0], inp["ev_a2"][0], inp["ev_g2"][0], inp["ev_k_k"][0], inp["ev_k_a"][0],
                          inp["ev_r_k"][0], inp["ev_pool_w"][0], inp["ev_pool_scale"][0])
    x = inp["x"].reshape(16384, D).astype(np.float32)
    ctx = inp["ctx"].reshape(256, D).astype(np.float32)
    lncol = lambda i: np.ascontiguousarray(np.stack([colmajor(inp["ln_g"][i, 0]), colmajor(inp["ln_b"][i, 0]), colmajor(inp["ln_g"][i, 1]), colmajor(inp["ln_b"][i, 1])], 1))
    shared = {"lnc0": lncol(0), "wout0": np.ascontiguousarray(inp["ev_w_out"][0]),
              "w1_0": np.ascontiguousarray(inp["ffn_w1"][0]), "w3_0": np.ascontiguousarray(inp["ffn_w3"][0]), "w2_0": np.ascontiguousarray(inp["ffn_w2"][0]),
              "gcols": np.ascontiguousarray(np.concatenate([colmajor(inp["ev_lnx_g"][0]), colmajor(inp["ev_lnx_b"][0])], 1)),
              "adaw1": np.ascontiguousarray(inp["ada_w"][1]), "adab1": colmajor(inp["ada_b"][1]), "lnc1": lncol(1),
              "w1_1": np.ascontiguousarray(inp["ffn_w1"][1]), "w3_1": np.ascontiguousarray(inp["ffn_w3"][1]), "w2_1": np.ascontiguousarray(inp["ffn_w2"][1]),
              "win1": np.ascontiguousarray(inp["od_w_in"][0]), "wout1": np.ascontiguousarray(inp["od_w_out"][0]), "tab": host_C_consts(inp["od_rpb"][0])}
    ren = {"adaw": "adaw0", "adab": "adab0", "win": "win0"}
    maps = []
    for k in range(8):
        m = {ren.get(n, n): v for n, v in mapsA[k].items()}
        m.update(shared)
        m["xin"] = np.ascontiguousarray(np.concatenate([x[k * 2048:(k + 1) * 2048], ctx], 0))
        cmask = np.zeros((128, 16), np.float32)
        hmask = np.zeros((128, 16), np.float32)
        for s in range(8):
            cmask[:, s] = 1.0 if s < k else 0.0
            cmask[:, 8 + s] = 1.0 if s > k else 0.0
            hmask[:, s] = 1.0 if s == k - 1 else 0.0
            hmask[:, 8 + s] = 1.0 if s == k + 1 else 0.0
        m["cmask"] = cmask
        m["hmask"] = hmask
        m["rowmask"] = host_C_rowmask(k)
        maps.append(m)
    if "F" not in _NC_CACHE:
        _NC_CACHE["F"] = build_fused()
    res = run_bass_kernel_spmd(_NC_CACHE["F"], maps, core_ids=list(range(8))).results
    return np.concatenate([r["out"] for r in res], 0).reshape(1, 16384, D).astype(np.float32)
```
